# Optimizing a Trainium2 kernel written in Bass

```python
import math
import jax, jax.numpy as jnp
from jax import lax
import numpy as np


D_MODEL = 1024
BATCH = 2
SEQ = 8192
DEPTH = 4
DEC_BATCH = 1
DEC_SEQ = 16384
PAST_LEN = 128

HEAD_DIM = 64
A_HEADS = 8
A_KV_HEADS = 2
A_GROUP = A_HEADS // A_KV_HEADS
B_HEADS = 8
MIX_WIDTH = (A_HEADS + B_HEADS) * HEAD_DIM
WINDOW = 128
BLK = 128
GRID_W = 64
NA_ROWS = 8
NA_COLS = 16
D_FF = 2816
PLE_DIM = 256
EPS = 1e-6

QA_W = A_HEADS * HEAD_DIM
KVA_W = A_KV_HEADS * HEAD_DIM
QB_W = B_HEADS * HEAD_DIM
IN_WIDTH = QA_W + 2 * KVA_W + 3 * QB_W
IN_SPLITS = [QA_W, QA_W + KVA_W, QA_W + 2 * KVA_W, QA_W + 2 * KVA_W + QB_W, QA_W + 2 * KVA_W + 2 * QB_W]

N_FFN1_PRE, N_FFN1_POST, N_MIX_PRE, N_MIX_POST, N_FFN2_PRE, N_FFN2_POST, N_PLE_PRE, N_PLE_POST, N_GROUP = range(9)
N_NORMS = 9

kernel_name = 'hymba_parallel_window_gqa_natten_macaron_encoder'


def rmsnorm(x, g):
    x32 = x.astype(jnp.float32)
    y = x32 * lax.rsqrt(jnp.mean(x32 * x32, axis=-1, keepdims=True) + EPS)
    return (y * g.astype(jnp.float32)).astype(x.dtype)


def swiglu(x, wg, wu, wd):
    return (jax.nn.silu(x @ wg) * (x @ wu)) @ wd


def alibi_slopes(n):
    return jnp.exp2(-(8.0 / n) * jnp.arange(1, n + 1, dtype=jnp.float32))


def window_attention(q, k, v, sink):
    B, L = q.shape[0], q.shape[1]
    nb = L // BLK
    qb = q.reshape(B, nb, BLK, A_KV_HEADS, A_GROUP, HEAD_DIM)

    def bands(t):
        tp = jnp.pad(t.reshape(B, L, A_KV_HEADS, HEAD_DIM), ((0, 0), (BLK, BLK), (0, 0), (0, 0)))
        tp = tp.reshape(B, nb + 2, BLK, A_KV_HEADS, HEAD_DIM)
        return jnp.concatenate([tp[:, :nb], tp[:, 1:nb + 1], tp[:, 2:]], axis=2)

    kb, vb = bands(k), bands(v)
    s = jnp.einsum('bnqkgd,bnskd->bnkgqs', qb, kb).astype(jnp.float32) * (HEAD_DIM ** -0.5)
    i = jnp.arange(BLK)
    j = jnp.arange(3 * BLK)
    rel = j[None, :] - BLK - i[:, None]
    dist = jnp.abs(rel).astype(jnp.float32)
    kpos = (jnp.arange(nb)[:, None] - 1) * BLK + j[None, :]
    mask = (jnp.abs(rel) <= WINDOW)[None] & ((kpos >= 0) & (kpos < L))[:, None, :]
    slopes = alibi_slopes(A_HEADS).reshape(A_KV_HEADS, A_GROUP)
    s = s - slopes[:, :, None, None] * dist
    s = jnp.where(mask[None, :, None, None], s, -jnp.inf)
    sk = sink.astype(jnp.float32).reshape(1, 1, A_KV_HEADS, A_GROUP, 1, 1)
    m = jnp.maximum(jnp.max(s, axis=-1, keepdims=True), sk)
    e = jnp.exp(s - m)
    pr = e / (jnp.sum(e, axis=-1, keepdims=True) + jnp.exp(sk - m))
    o = jnp.einsum('bnkgqs,bnskd->bnqkgd', pr.astype(v.dtype), vb)
    return o.reshape(B, L, QA_W)


def neighborhood_attention(q, k, v, rpb):
    B, L = q.shape[0], q.shape[1]
    rows = L // GRID_W
    kr = min(NA_ROWS, rows)
    qg = q.reshape(B, rows, GRID_W, B_HEADS, HEAD_DIM)
    kg = k.reshape(B, rows, GRID_W, B_HEADS, HEAD_DIM)
    vg = v.reshape(B, rows, GRID_W, B_HEADS, HEAD_DIM)
    r = jnp.arange(rows)
    row_idx = jnp.clip(r - kr // 2, 0, rows - kr)[:, None] + jnp.arange(kr)[None, :]
    kw = kg[:, row_idx]
    vw = vg[:, row_idx]
    s = jnp.einsum('brqhd,brjkhd->brhqjk', qg, kw).astype(jnp.float32) * (HEAD_DIM ** -0.5)
    c = jnp.arange(GRID_W)
    cs = jnp.clip(c - NA_COLS // 2, 0, GRID_W - NA_COLS)
    col_ok = (c[None, :] >= cs[:, None]) & (c[None, :] < cs[:, None] + NA_COLS)
    dr = row_idx - r[:, None] + (NA_ROWS - 1)
    dc = jnp.clip(c[None, :] - c[:, None], -(NA_COLS - 1), NA_COLS - 1) + (NA_COLS - 1)
    bias = rpb.astype(jnp.float32)[:, dr][:, :, :, dc]
    bias = bias.transpose(1, 0, 3, 2, 4)
    s = s + bias[None]
    s = jnp.where(col_ok[:, None, :], s, -jnp.inf)
    pr = jax.nn.softmax(s.reshape(B, rows, B_HEADS, GRID_W, kr * GRID_W), axis=-1).reshape(s.shape)
    o = jnp.einsum('brhqjk,brjkhd->brqhd', pr.astype(v.dtype), vw)
    return o.reshape(B, L, QB_W)


def mixing(u, w_in, w_out, sink, rpb, g_group):
    z = u @ w_in
    qa, ka, va, qb, kb, vb = jnp.split(z, IN_SPLITS, axis=-1)
    oa = rmsnorm(window_attention(qa, ka, va, sink), g_group[:QA_W])
    ob = rmsnorm(neighborhood_attention(qb, kb, vb, rpb), g_group[QA_W:])
    return jnp.concatenate([oa, ob], axis=-1) @ w_out


def trunk(x, p, w_ffn1_gate, w_ffn1_up, w_ffn1_down, w_in, w_out, sink, rpb,
          w_ffn2_gate, w_ffn2_up, w_ffn2_down, w_ple_proj, w_ple_gate, norm_g):
    h = x
    for i in range(DEPTH):
        g = norm_g[i]
        f1 = swiglu(rmsnorm(h, g[N_FFN1_PRE]), w_ffn1_gate[i], w_ffn1_up[i], w_ffn1_down[i])
        h = h + 0.5 * rmsnorm(f1, g[N_FFN1_POST])
        mix = mixing(rmsnorm(h, g[N_MIX_PRE]), w_in[i], w_out[i], sink[i], rpb[i], g[N_GROUP])
        h = h + rmsnorm(mix, g[N_MIX_POST])
        f2 = swiglu(rmsnorm(h, g[N_FFN2_PRE]), w_ffn2_gate[i], w_ffn2_up[i], w_ffn2_down[i])
        h = h + 0.5 * rmsnorm(f2, g[N_FFN2_POST])
        gate = jax.nn.sigmoid(rmsnorm(h, g[N_PLE_PRE]) @ w_ple_gate[i])
        h = h + rmsnorm((p[i] @ w_ple_proj[i]) * gate, g[N_PLE_POST])
    return h


def setup_inputs(seed: int = 0) -> dict:
    key = jax.random.key(seed)
    ks = jax.random.split(key, 20)
    f32 = jnp.float32

    def w(k, shape, fan_in):
        return jax.random.normal(k, shape, f32) * (fan_in ** -0.5)

    return {
        'x_prompt': jax.random.normal(ks[0], (BATCH, SEQ, D_MODEL), f32),
        'x_sample': jax.random.normal(ks[1], (DEC_BATCH, DEC_SEQ, D_MODEL), f32),
        'p_prompt': jax.random.normal(ks[2], (DEPTH, BATCH, SEQ, PLE_DIM), f32),
        'p_sample': jax.random.normal(ks[3], (DEPTH, DEC_BATCH, DEC_SEQ, PLE_DIM), f32),
        'w_ffn1_gate': w(ks[4], (DEPTH, D_MODEL, D_FF), D_MODEL),
        'w_ffn1_up': w(ks[5], (DEPTH, D_MODEL, D_FF), D_MODEL),
        'w_ffn1_down': w(ks[6], (DEPTH, D_FF, D_MODEL), D_FF),
        'w_in': w(ks[7], (DEPTH, D_MODEL, IN_WIDTH), D_MODEL),
        'w_out': w(ks[8], (DEPTH, MIX_WIDTH, D_MODEL), MIX_WIDTH),
        'sink': jax.random.normal(ks[9], (DEPTH, A_HEADS), f32) * 0.5,
        'rpb': jax.random.normal(ks[10], (DEPTH, B_HEADS, 2 * NA_ROWS - 1, 2 * NA_COLS - 1), f32) * 0.1,
        'w_ffn2_gate': w(ks[11], (DEPTH, D_MODEL, D_FF), D_MODEL),
        'w_ffn2_up': w(ks[12], (DEPTH, D_MODEL, D_FF), D_MODEL),
        'w_ffn2_down': w(ks[13], (DEPTH, D_FF, D_MODEL), D_FF),
        'w_ple_proj': w(ks[14], (DEPTH, PLE_DIM, D_MODEL), PLE_DIM),
        'w_ple_gate': w(ks[15], (DEPTH, D_MODEL, D_MODEL), D_MODEL),
        'norm_g': 1.0 + 0.01 * jax.random.normal(ks[16], (DEPTH, N_NORMS, D_MODEL), f32),
    }


def reference(x_prompt, x_sample, p_prompt, p_sample, w_ffn1_gate, w_ffn1_up, w_ffn1_down,
              w_in, w_out, sink, rpb, w_ffn2_gate, w_ffn2_up, w_ffn2_down,
              w_ple_proj, w_ple_gate, norm_g):
    y_prompt = trunk(x_prompt, p_prompt, w_ffn1_gate, w_ffn1_up, w_ffn1_down, w_in, w_out, sink, rpb,
                     w_ffn2_gate, w_ffn2_up, w_ffn2_down, w_ple_proj, w_ple_gate, norm_g)
    y_sample = trunk(x_sample, p_sample, w_ffn1_gate, w_ffn1_up, w_ffn1_down, w_in, w_out, sink, rpb,
                     w_ffn2_gate, w_ffn2_up, w_ffn2_down, w_ple_proj, w_ple_gate, norm_g)
    return (y_prompt, y_sample)
```

```python
import contextlib
import os

import numpy as np

import concourse.bass as bass
import concourse.mybir as mybir
from concourse.bass_utils import run_bass_kernel_spmd

F32 = mybir.dt.float32
BF16 = mybir.dt.bfloat16
AF = mybir.ActivationFunctionType
ALU = mybir.AluOpType
AX = mybir.AxisListType

D = 1024
DFF = 2816
NL = 4
HD = 64
NTOK = 4096
HALO = 1024
NLOC = NTOK + 2 * HALO
TT = 512
KC = 8
FC = 22
EPS = 1e-6
NEG = -30000.0
NPAIR = NLOC // 128
NOFF = 7
(N_FFN1_PRE, N_FFN1_POST, N_MIX_PRE, N_MIX_POST, N_FFN2_PRE, N_FFN2_POST,
 N_PLE_PRE, N_PLE_POST, N_GROUP) = range(9)

SAME_ENGINE_SYNC = os.environ.get("K_SAMESYNC", "1") == "1"
RING = 5
SLOT_E = 2048

E_GU, E_D, E_FM, E_WO, E_PG = 2048, 1408, 1024, 1024, 1280
A_SLABS = []
_off = 0
for _i in range(FC):
    A_SLABS.append(("gu", _i, _off, E_GU, 1)); _off += 128 * E_GU
for _i in range(16):
    A_SLABS.append(("d", _i, _off, E_D, 1)); _off += 128 * E_D
for _i in range(0, 12, 2):
    A_SLABS.append(("fm", _i, _off, 2 * E_FM, 2)); _off += 2 * 128 * E_FM
A_SLABS.append(("fm", 12, _off, E_FM, 1)); _off += 128 * E_FM
for _i, _e in enumerate((2048, 2048, 1024)):
    A_SLABS.append(("wv", _i, _off, _e, 1)); _off += 128 * _e
A_SIZE = _off
B_SLABS = []
_off = 0
for _i in range(0, 8, 2):
    B_SLABS.append(("wo", _i, _off, 2 * E_WO, 2)); _off += 2 * 128 * E_WO
for _i in range(FC):
    B_SLABS.append(("gu", _i, _off, E_GU, 1)); _off += 128 * E_GU
for _i in range(16):
    B_SLABS.append(("d", _i, _off, E_D, 1)); _off += 128 * E_D
for _i in range(8):
    B_SLABS.append(("pg", _i, _off, E_PG, 1)); _off += 128 * E_PG
B_SIZE = _off
assert A_SIZE == B_SIZE
GRP = A_SIZE
WTOT = NL * 2 * GRP
CV_PIECE = 128 * 2688
assert GRP % CV_PIECE == 0


class Sched:
    def __init__(self, nc):
        self.nc = nc
        self.ops = []
        self.tag = ""

    def add(self, eng, fn, r=(), w=(), dma=None, group=False, final=False):
        self.ops.append(dict(eng=eng, fn=fn, r=tuple(r), w=tuple(w), dma=dma, group=group, ms=False, final=final, tag=self.tag))

    def emit(self, es):
        nc = self.nc
        engs = {"pe": nc.tensor, "act": nc.scalar, "dve": nc.vector, "pool": nc.gpsimd, "sp": nc.sync}
        ops = self.ops
        last_w, readers, last_dma, last_any_dma = {}, {}, {}, {}
        known = {e: {} for e in engs}
        known_dma = {e: set() for e in engs}
        for i, op in enumerate(ops):
            deps = set()
            for b in op["r"]:
                if b in last_w:
                    deps.add(last_w[b])
            for b in op["w"]:
                if b in last_w:
                    deps.add(last_w[b])
                deps.update(readers.get(b, ()))
            if op["final"]:
                deps.update(last_any_dma.values())
            if op["dma"]:
                last_any_dma[op["dma"]] = i
            if op["dma"] and not op["group"]:
                p = last_dma.get(op["dma"])
                if p is not None:
                    deps.add(p)
                last_dma[op["dma"]] = i
            for b in op["r"]:
                if not (isinstance(b, str) and b.startswith("c:")):
                    readers.setdefault(b, []).append(i)
            for b in op["w"]:
                last_w[b] = i
                readers[b] = []
            e = op["eng"]
            waits, best = [], {}
            for d in deps:
                pd = ops[d]
                if pd["dma"]:
                    if d not in known_dma[e]:
                        known_dma[e].add(d)
                        waits.append(d)
                else:
                    pe_ = pd["eng"]
                    if pe_ == e and not op["dma"] and (e == "pe" or not SAME_ENGINE_SYNC):
                        continue
                    if d > best.get(pe_, -1):
                        best[pe_] = d
            for pe_, d in best.items():
                if known[e].get(pe_, -1) >= d:
                    continue
                known[e][pe_] = d
                waits.append(d)
            op["waits"] = waits
            for d in waits:
                ops[d]["ms"] = True
        gtot = {}
        for op in ops:
            if op["dma"] and op["group"]:
                gtot[op["dma"]] = gtot.get(op["dma"], 0) + 16
        sems = {e: es.enter_context(nc.semaphore("s_" + e)) for e in engs}
        cnt = {e: 0 for e in engs}
        dsem, dcnt, token = {}, {}, {}
        for i, op in enumerate(ops):
            e = engs[op["eng"]]
            for d in op["waits"]:
                s, v = token[d]
                e.wait_ge(s, v)
            inst = op["fn"](e)
            if op["dma"]:
                name = op["dma"]
                if name not in dsem:
                    dsem[name] = es.enter_context(nc.semaphore("d_" + name))
                    dcnt[name] = 0
                dcnt[name] += 16
                inst.then_inc(dsem[name], 16)
                token[i] = (dsem[name], gtot[name] if op["group"] else dcnt[name])
            elif op["ms"]:
                cnt[op["eng"]] += 1
                inst.then_inc(sems[op["eng"]], 1)
                token[i] = (sems[op["eng"]], cnt[op["eng"]])


def build_program(n_layers=NL, stage=None, debug=False):
    nc = bass.Bass("TRN2", target_bir_lowering=False)
    es = contextlib.ExitStack()
    S = Sched(nc)

    def dram(name, shape, dt, kind):
        return nc.dram_tensor(name, shape, dt, kind=kind).ap()

    xT = dram("xT", [D, NLOC], F32, "ExternalInput")
    pT = dram("pT", [NL * 256, NLOC], F32, "ExternalInput")
    wall = dram("wall", [WTOT], F32, "ExternalInput")
    gn_d = dram("gn", [128, NL * 9 * 8], F32, "ExternalInput")
    sink_d = dram("sinkb", [128, NL * 8], F32, "ExternalInput")
    rpb_d = dram("rpbp", [NL * 120, 191], F32, "ExternalInput")
    wtab_d = dram("wtab", [128, 3 * 8 * 128], F32, "ExternalInput")
    cmask_d = dram("cmask", [128, 64], F32, "ExternalInput")
    ident_d = dram("identf", [128, 128], F32, "ExternalInput")
    nab_d = dram("nab", [128, NPAIR * NOFF * 2], F32, "ExternalInput")
    wkb_d = dram("wkb", [128, NPAIR], F32, "ExternalInput")
    yT = dram("yT", [D, NTOK], F32, "ExternalOutput")
    wbf = dram("wbf", [WTOT], BF16, "Internal")
    pbf = dram("pbf", [NL * 256, NLOC], BF16, "Internal")
    hs = dram("hs", [D, NLOC], F32, "ExternalOutput" if debug else "Internal")

    def sb(name, shape, dt):
        return es.enter_context(nc.sbuf_tensor(name, shape, dt))

    hA = sb("hA", [128, KC, TT], F32)
    hB = sb("hB", [128, KC, TT], F32)
    xn = sb("xn", [128, KC, TT], BF16)
    sq = sb("sq", [128, KC, TT], BF16)
    act = sb("act", [128, FC, TT], BF16)
    fo = sb("fo", [128, KC, TT], F32)
    ssv = [sb(f"ssv{i}", [128, TT], F32) for i in range(2)]
    rstd = [sb(f"rstd{i}", [128, TT], F32) for i in range(2)]
    sg = [sb(f"sg{i}", [128, TT], F32) for i in range(2)]
    QT = [sb(f"QT{i}", [128, 8, TT], BF16) for i in range(2)]
    QW = [QT[i][:, 0:4, :].rearrange("p a t -> p (a t)").rearrange("p (s c q) -> p s c q", s=4, c=4) for i in range(2)]
    KT = [sb(f"KT{i}", [128, 5, TT], BF16) for i in range(2)]
    V = [sb(f"V{i}", [128, 4, 10, 65], BF16) for i in range(2)]
    NE = 3
    NP_ = int(os.environ.get("K_NP", "4"))
    NSB = int(os.environ.get("K_NSB", "4"))
    NZ = int(os.environ.get("K_NZ", "2"))
    QZw = [sb(f"QZw{i}", [128, 2, 512], BF16) for i in range(NZ)]
    QZn = [sb(f"QZn{i}", [128, 4, 2, 128], BF16) for i in range(NZ)]
    Eb = [sb(f"E{i}", [128, 512], F32) for i in range(NE)]
    Pb = [sb(f"P{i}", [128, 512], BF16) for i in range(NP_)]
    On = [sb(f"On{i}", [128, 512], F32) for i in range(2)]
    Obf = sb("Obf", [128, 1024], BF16)
    small = sb("small", [128, 64], F32)
    wt_tab = sb("wt_tab", [128, 3 * 8 * 128], BF16)
    Gt = sb("Gt", [128, NOFF * 8 * 128], BF16)
    pTt = sb("pTt", [128, 2, TT], BF16)
    ws = [sb(f"ws{i}", [128, SLOT_E], BF16) for i in range(RING)]
    gn = sb("gn_s", [128, NL * 9 * 8], F32)
    g5 = sb("g5_s", [128, NL * 9 * 8], F32)
    esink = sb("esink", [128, NL * 8], F32)
    cmask = sb("cmask_s", [128, 64], F32)
    identf = sb("identf_s", [128, 128], F32)
    identb = sb("identb_s", [128, 128], BF16)
    ones = sb("ones_s", [128, 128], BF16)
    nhalf = sb("nhalf_s", [128, 1], F32)
    epst = sb("eps_s", [128, 1], F32)
    nab = sb("nab_s", [128, NPAIR * NOFF * 2], F32)
    wkb = sb("wkb_s", [128, NPAIR], F32)
    R2 = sb("R2", [120, 191], F32)
    _ps03 = [es.enter_context(nc.psum_tensor("ps%d" % i, [128, 512], F32)) for i in range(4)]
    psS = es.enter_context(nc.psum_tensor("psS", [128, 3, 512], F32))
    _ps7 = es.enter_context(nc.psum_tensor("ps7", [128, 512], F32))
    ps = [_ps03[0][:], _ps03[1][:], _ps03[2][:], _ps03[3][:], psS[:, 0, :], psS[:, 1, :], psS[:, 2, :], _ps7[:]]
    psH = psS[:].rearrange("p b (h x) -> p (b h) x", h=2)

    def pn(b):
        return [("psh", b, 0), ("psh", b, 1)] if 4 <= b <= 6 else ["ps%d" % b]

    def gcol(l, n, c):
        i = (l * 9 + n) * 8 + c
        return gn[:, i:i + 1]

    def g5col(l, n, c):
        i = (l * 9 + n) * 8 + c
        return g5[:, i:i + 1]

    PRO = os.environ.get("K_PRO", "abcdefgh")
    S.add("pool", lambda e: e.memset(ones[:], 1.0), w=["c:ones"])
    S.add("pool", lambda e: e.memset(nhalf[:], -0.5), w=["c:nh"])
    S.add("pool", lambda e: e.memset(epst[:], EPS), w=["c:eps"])
    for z_ in range(NZ):
        S.add("pool", lambda e, z_=z_: e.memset(QZw[z_][:], 0.0), w=[("QZw", z_)])
        S.add("pool", lambda e, z_=z_: e.memset(QZn[z_][:], 0.0), w=[("QZn", z_)])
    for s_ in (range(2) if "a" in PRO else []):
        S.add("pool", lambda e, s_=s_: e.memset(V[s_][:, :, :, 64:65], 1.0),
              w=[("V", s_, sub, g) for sub in range(4) for g in range(3)])
    S.add("sp", lambda e: e.dma_start(out=gn[:], in_=gn_d[:, :]), w=["c:gn"], dma="c0")
    S.add("sp", lambda e: e.dma_start(out=esink[:], in_=sink_d[:, :]), w=["esink_raw"], dma="c1")
    S.add("sp", lambda e: e.dma_start(out=cmask[:], in_=cmask_d[:, :]), w=["c:cmask"], dma="c2")
    S.add("sp", lambda e: e.dma_start(out=identf[:], in_=ident_d[:, :]), w=["c:identf"], dma="c3")
    if "b" in PRO:
        S.add("sp", lambda e: e.dma_start(out=nab[:], in_=nab_d[:, :]), w=["c:nab"], dma="c4")
        S.add("sp", lambda e: e.dma_start(out=wkb[:], in_=wkb_d[:, :]), w=["c:wkb"], dma="c5")
    if "c" in PRO:
        S.add("pool", lambda e: e.dma_start(out=identb[:], in_=ident_d[:, :]), w=["c:identb"], dma="c6")
        S.add("pool", lambda e: e.dma_start(out=wt_tab[:], in_=wtab_d[:, :]), w=["c:wtab"], dma="c7")
    if "d" in PRO:
        S.add("act", lambda e: e.activation(out=esink[:], in_=esink[:], func=AF.Exp), r=["esink_raw"], w=["c:esink"])
    if "e" in PRO:
        S.add("dve", lambda e: e.tensor_scalar(out=g5[:], in0=gn[:], scalar1=0.5, scalar2=None, op0=ALU.mult),
              r=["c:gn"], w=["c:g5"])
    cv_pending = []
    cv_done = set()
    cv_state = {"n": 0}

    def cv_piece(l, ph, k):
        o = (l * 2 + ph) * GRP + k * CV_PIECE
        st_ = f"cv{cv_state['n'] % 2}"
        cv_state["n"] += 1
        cv_done.add((l, ph, k))
        S.add("pool",
              lambda e: e.dma_start(
                  out=wbf[o:o + CV_PIECE].rearrange("(p x) -> p x", p=128),
                  in_=wall[o:o + CV_PIECE].rearrange("(p x) -> p x", p=128)),
              w=[("wbf", l, ph, k)], dma=st_)

    def cv_pbf(l):
        st_ = f"cv{cv_state['n'] % 2}"
        cv_state["n"] += 1
        S.add("pool", lambda e: e.dma_start(out=pbf[l * 256:(l + 1) * 256, :], in_=pT[l * 256:(l + 1) * 256, :]),
              w=[("pbf", l)], dma=st_)

    for l in range(n_layers):
        for ph in range(2):
            for k in range(GRP // CV_PIECE):
                if l == 0:
                    cv_piece(l, ph, k)
                else:
                    cv_pending.append(lambda l=l, ph=ph, k=k: cv_piece(l, ph, k))
            if ph == 0:
                if l == 0:
                    cv_pbf(l)
                else:
                    cv_pending.append(lambda l=l: cv_pbf(l))

    def cv_hook(nmax=1):
        tg = S.tag
        S.tag = "cv"
        for _ in range(nmax):
            if cv_pending:
                cv_pending.pop(0)()
        S.tag = tg

    order = []
    for l in range(n_layers):
        nA = 12 - l
        if stage in ("P", "G"):
            break
        if isinstance(stage, str) and stage.startswith("A0"):
            order += [("A", l, t) for t in range(nA if stage == "A0" else int(stage[2:]))]
            break
        order.append(("A", l, 0))
        for t in range(1, nA):
            order += [("A", l, t), ("B", l, t - 1)]
    seq = []
    for ph, l, t in order:
        tab = A_SLABS if ph == "A" else B_SLABS
        for si in range(len(tab)):
            seq.append((l, 0 if ph == "A" else 1, si))
    wstate = {"next_use": 0, "next_load": 0}

    def emit_load(n):
        if n >= len(seq):
            return
        l, ph, si = seq[n]
        tab = A_SLABS if ph == 0 else B_SLABS
        _, _, off, E, nsub = tab[si]
        o = (l * 2 + ph) * GRP + off
        slot = n % RING
        for k in range(off // CV_PIECE, (off + 128 * E - 1) // CV_PIECE + 1):
            assert (l, ph, k) in cv_done, ("weight piece not converted yet", l, ph, k)
        if nsub == 1:
            fn = (lambda e, o=o, E=E, slot=slot: e.dma_start(
                out=ws[slot][:, 0:E], in_=wbf[o:o + 128 * E].rearrange("(p x) -> p x", p=128)))
        else:
            fn = (lambda e, o=o, E=E, slot=slot, nsub=nsub: e.dma_start(
                out=ws[slot][:, 0:E].rearrange("p (s x) -> p s x", s=nsub),
                in_=wbf[o:o + 128 * E].rearrange("(s p x) -> p s x", s=nsub, p=128)))
        S.add("sp", fn,
              r=[("wbf", l, ph, k) for k in range(off // CV_PIECE, (off + 128 * E - 1) // CV_PIECE + 1)],
              w=[("ws", slot)], dma=f"w{slot}")

    def use_slab(kind, idx):
        cu = wstate.get("cur")
        if cu is not None and cu[0] == kind and cu[1] < idx < cu[1] + cu[2]:
            slot, esub = cu[3], cu[4]
            sub = idx - cu[1]
            return ws[slot][:, sub * esub:(sub + 1) * esub], ("ws", slot)
        n = wstate["next_use"]
        l, ph, si = seq[n]
        tab = A_SLABS if ph == 0 else B_SLABS
        assert tab[si][0] == kind and tab[si][1] == idx, (tab[si], kind, idx)
        if n == 0:
            for k in range(RING):
                emit_load(k)
        else:
            emit_load(n - 1 + RING)
        wstate["next_use"] = n + 1
        slot = n % RING
        E, nsub = tab[si][3], tab[si][4]
        wstate["cur"] = (kind, idx, nsub, slot, E // nsub)
        return ws[slot][:, 0:E // nsub], ("ws", slot)

    ILSTAT = os.environ.get("K_ILSTAT", "1") == "1"
    rot = {"ssv": 0, "sg": 0, "E": 0, "P": 0, "On": 0, "SB": 0, "SN": 0, "Z": 0}
    TRDELAY = int(os.environ.get("K_TRDELAY", "6"))
    NAMERGE = os.environ.get("K_NAMERGE", "1") == "1"

    def nxt(k, n):
        v = rot[k]
        rot[k] = (v + 1) % n
        return v

    def stat_mm(c):
        S.add("pe", lambda e, c=c: e.matmul(ps[6][:], ones[:], sq[:, c, :], start=(c == 0), stop=(c == KC - 1)),
              r=["c:ones", ("sq", c)], w=[*pn(6)])

    def stats_rstd(sq_names, done=0):
        S.tag = S.tag.split("/")[0] + "/stats"
        for c in range(done, KC):
            stat_mm(c)
        i = nxt("ssv", 2)
        S.add("act", lambda e, i=i: e.activation(out=ssv[i][:], in_=ps[6][:], func=AF.Sqrt, scale=1.0 / D, bias=epst[:, 0:1]),
              r=[*pn(6), "c:eps"], w=[("ssv", i)])
        S.add("dve", lambda e, i=i: e.reciprocal(out=rstd[i][:], in_=ssv[i][:]),
              r=[("ssv", i)], w=[("rstd", i)])
        return i

    def prenorm(hbuf, hname, l, nidx):
        for c in range(KC):
            S.add("act", lambda e, c=c: e.activation(out=sq[:, c, :], in_=hbuf[:, c, :], func=AF.Square),
                  r=[(hname, c)], w=[("sq", c)])
        i = stats_rstd([("sq", c) for c in range(KC)])
        for c in range(KC):
            S.add("dve", lambda e, c=c, i=i: e.scalar_tensor_tensor(
                out=xn[:, c, :], in0=hbuf[:, c, :], scalar=gcol(l, nidx, c), in1=rstd[i][:],
                op0=ALU.mult, op1=ALU.mult),
                r=[(hname, c), ("rstd", i), "c:gn"], w=[("xn", c)])

    def postnorm_update(hbuf, hname, l, nidx, half, hook=None, done=0):
        i = stats_rstd([("sq", c) for c in range(KC)], done)
        if hook is not None:
            tg = S.tag
            hook()
            S.tag = tg
        for c in range(KC):
            S.add("dve", lambda e, c=c, i=i: e.tensor_tensor(out=fo[:, c, :], in0=fo[:, c, :], in1=rstd[i][:], op=ALU.mult),
                  r=[("fo", c), ("rstd", i)], w=[("fo", c)])
        for c in range(KC):
            S.add("dve", lambda e, c=c: e.tensor_tensor(out=hbuf[:, c, :], in0=hbuf[:, c, :], in1=fo[:, c, :], op=ALU.add),
                  r=[("fo", c), (hname, c)], w=[(hname, c)])

    def ffn(hbuf, hname, l, n_pre, n_post, skip_pre=False, mid_hook=None):
        S.tag = hname + ".ffn_pre"
        if not skip_pre:
            prenorm(hbuf, hname, l, n_pre)
        S.tag = hname + ".ffn_gu"
        for fc in range(FC):
            if fc == 8 and mid_hook is not None:
                mid_hook()
            wsl, wn = use_slab("gu", fc)
            pg_, pu_ = fc % 2, 2 + fc % 2
            for kc in range(KC):
                S.add("pe", lambda e, kc=kc, wsl=wsl, pg_=pg_: e.matmul(
                    ps[pg_][:], wsl[:, kc * 128:(kc + 1) * 128], xn[:, kc, :], start=(kc == 0), stop=(kc == KC - 1)),
                    r=[wn, ("xn", kc)], w=[*pn(pg_)])
            for kc in range(KC):
                S.add("pe", lambda e, kc=kc, wsl=wsl, pu_=pu_: e.matmul(
                    ps[pu_][:], wsl[:, 1024 + kc * 128:1024 + (kc + 1) * 128], xn[:, kc, :],
                    start=(kc == 0), stop=(kc == KC - 1)),
                    r=[wn, ("xn", kc)], w=[*pn(pu_)])
            j = nxt("sg", 2)
            S.add("act", lambda e, j=j, pg_=pg_: e.activation(out=sg[j][:], in_=ps[pg_][:], func=AF.Silu),
                  r=[*pn(pg_)], w=[("sg", j)])
            S.add("dve", lambda e, j=j, pu_=pu_, fc=fc: e.tensor_tensor(out=act[:, fc, :], in0=ps[pu_][:], in1=sg[j][:], op=ALU.mult),
                  r=[*pn(pu_), ("sg", j)], w=[("act", fc)])
        cv_hook()
        S.tag = hname + ".ffn_down"
        for dc in range(KC):
            pd_ = 4 + dc % 2
            for hf in range(2):
                wsl, wn = use_slab("d", dc * 2 + hf)
                for j in range(11):
                    f = hf * 11 + j
                    S.add("pe", lambda e, j=j, f=f, wsl=wsl, pd_=pd_: e.matmul(
                        ps[pd_][:], wsl[:, j * 128:(j + 1) * 128], act[:, f, :], start=(f == 0), stop=(f == FC - 1)),
                        r=[wn, ("act", f)], w=[*pn(pd_)])
            if dc >= 1 and ILSTAT:
                tg_ = S.tag
                S.tag = hname + ".ffn_post/stats"
                stat_mm(dc - 1)
                S.tag = tg_
            S.add("act", lambda e, dc=dc, pd_=pd_: e.activation(out=fo[:, dc, :], in_=ps[pd_][:], func=AF.Copy, scale=g5col(l, n_post, dc)),
                  r=[*pn(pd_), "c:g5"], w=[("fo", dc)])
            S.add("act", lambda e, dc=dc, pd_=pd_: e.activation(out=sq[:, dc, :], in_=ps[pd_][:], func=AF.Square),
                  r=[*pn(pd_)], w=[("sq", dc)])
        cv_hook()
        S.tag = hname + ".ffn_post"
        postnorm_update(hbuf, hname, l, n_post, True, done=(KC - 1 if ILSTAT else 0))

    def tok_ap(dr, t0):
        return dr[:, t0:t0 + TT].rearrange("(c p) t -> p c t", p=128)

    def hs_names(t0):
        return [("hs", t0 // 256), ("hs", t0 // 256 + 1)]

    def load_hA(l, t):
        t0 = 256 * l + TT * t
        src = xT if l == 0 else hs
        S.add("sp", lambda e: e.dma_start(out=hA[:], in_=tok_ap(src, t0)),
              r=([] if l == 0 else hs_names(t0)), w=[("hA", c) for c in range(KC)], dma="hA")

    def phase_A(l, t, preloaded=False, prenormed=False, mid_hook=None):
        t0 = 256 * l + TT * t
        slot = t % 2
        if not preloaded:
            load_hA(l, t)
        ffn(hA, "hA", l, N_FFN1_PRE, N_FFN1_POST, skip_pre=prenormed, mid_hook=mid_hook)
        def store_A():
            tg = S.tag
            S.tag = "st"
            S.add("sp", lambda e: e.dma_start(out=tok_ap(hs, t0), in_=hA[:]),
                  r=[("hA", c) for c in range(KC)], w=hs_names(t0), dma="stA")
            S.tag = tg
        cv_hook()
        S.tag = "A.mixpre"
        prenorm(hA, "hA", l, N_MIX_PRE)
        S.tag = "A.qk"
        for si in range(13):
            if si == 5:
                store_A()
            wsl, wn = use_slab("fm", si)
            pb = 4 + si % 2
            for kc in range(KC):
                S.add("pe", lambda e, kc=kc, wsl=wsl, pb=pb: e.matmul(
                    ps[pb][:], wsl[:, kc * 128:(kc + 1) * 128], xn[:, kc, :], start=(kc == 0), stop=(kc == KC - 1)),
                    r=[wn, ("xn", kc)], w=[*pn(pb)])
            if si < 4:
                dst, dn = QW[slot][:, :, si, :], ("QT", slot, si)
            elif si == 4:
                dst, dn = KT[slot][:, 0, :], ("KT", slot, 0)
            elif si < 9:
                dst, dn = QT[slot][:, si - 1, :], ("QT", slot, si - 1)
            else:
                dst, dn = KT[slot][:, si - 8, :], ("KT", slot, si - 8)
            eng = "act" if si % 2 == 0 else "dve"
            srcp = ps[pb][:].rearrange("p (s q) -> p s q", s=4) if si < 4 else ps[pb][:]
            if eng == "act":
                S.add("act", lambda e, dst=dst, srcp=srcp: e.activation(out=dst, in_=srcp, func=AF.Copy),
                      r=[*pn(pb)], w=[dn])
            else:
                S.add("dve", lambda e, dst=dst, srcp=srcp: e.tensor_copy(out=dst, in_=srcp),
                      r=[*pn(pb)], w=[dn])
        cnt = 0
        S.tag = "A.v"
        for g in range(3):
            wsl, wn = use_slab("wv", g)
            ncol = 256 if g < 2 else 128
            for sub in range(4):
                pb = 4 + cnt % 2
                cnt += 1
                for kc in range(KC):
                    S.add("pe", lambda e, kc=kc, wsl=wsl, pb=pb, sub=sub, ncol=ncol: e.matmul(
                        ps[pb][:, 0:ncol], xn[:, kc, sub * 128:(sub + 1) * 128], wsl[:, kc * ncol:(kc + 1) * ncol],
                        start=(kc == 0), stop=(kc == KC - 1)),
                        r=[wn, ("xn", kc)], w=[*pn(pb)])
                nh = ncol // 64
                h0 = 2 + 4 * g if g < 2 else 0
                dst = V[slot][:, sub, h0:h0 + nh, 0:64]
                srcp = ps[pb][:, 0:ncol].rearrange("p (h d) -> p h d", d=64)
                if cnt % 2 == 0:
                    S.add("act", lambda e, dst=dst, srcp=srcp: e.activation(out=dst, in_=srcp, func=AF.Copy),
                          r=[*pn(pb)], w=[("V", slot, sub, g)])
                else:
                    S.add("dve", lambda e, dst=dst, srcp=srcp: e.tensor_copy(out=dst, in_=srcp),
                          r=[*pn(pb)], w=[("V", slot, sub, g)])

    def build_G(l):
        S.tag = "G"
        S.add("sp", lambda e: e.dma_start(out=R2[:], in_=rpb_d[l * 120:(l + 1) * 120, :]), w=["R2"], dma="r2")
        texp = act[:, 0:15, :].rearrange("p a b -> p (a b)").rearrange("p (m q) -> p m q", q=64)
        for q4 in range(16):
            pb = q4 % 2
            for k in range(4):
                qc = q4 * 4 + k
                S.add("pe", lambda e, qc=qc, k=k, pb=pb: e.transpose(
                    ps[pb][:, k * 120:(k + 1) * 120], R2[:, 63 - qc:191 - qc], identf[0:120, 0:120]),
                    r=["R2", "c:identf"], w=[*pn(pb)])
                S.add("pe", lambda e, qc=qc, k=k, pb=pb: e.transpose(
                    ps[pb][0:64, k * 120:(k + 1) * 120], R2[:, 127 - qc:191 - qc], identf[0:120, 0:120]),
                    r=["R2", "c:identf"], w=[*pn(pb)])
            dst = texp[:, :, q4 * 4:q4 * 4 + 4].rearrange("p m q -> p q m")
            srcp = ps[pb][:, 0:480].rearrange("p (q m) -> p q m", m=120)
            S.add("act", lambda e, dst=dst, srcp=srcp: e.activation(out=dst, in_=srcp, func=AF.Exp),
                  r=[*pn(pb)], w=[("act", c) for c in range(15)])
        Gv = Gt[:].rearrange("p (o h i q) -> p o h i q", o=NOFF, h=8, i=2)
        tv = act[:, 0:15, :].rearrange("p a b -> p (a b)").rearrange("p (h r q) -> p h r q", h=8, r=15)
        k = 0
        for oi in range(NOFF):
            o = oi - 3
            for i_ in range(2):
                for j in range(2):
                    dr = 2 * o + j - i_ + 7
                    psl = slice(64 * j, 64 * j + 64)
                    dst = Gv[psl, oi, :, i_, :]
                    eng = "dve"
                    k += 1
                    if 0 <= dr <= 14:
                        srcp = tv[psl, :, dr, :]
                        cm = cmask[psl, :].unsqueeze(1).to_broadcast([64, 8, 64])
                        S.add(eng, lambda e, dst=dst, srcp=srcp, cm=cm: e.tensor_tensor(out=dst, in0=srcp, in1=cm, op=ALU.mult),
                              r=[("act", c) for c in range(15)] + ["c:cmask"], w=[("G", oi, i_, j)])
                    else:
                        S.add(eng, lambda e, dst=dst: e.memset(dst, 0.0), w=[("G", oi, i_, j)])

    def attn_finish(l, typ, banks, obase):
        i = nxt("On", 2)
        tg = S.tag
        den = small[:, 16 * typ:16 * typ + 8]
        rden = small[:, 16 * typ + 8:16 * typ + 16]
        for b in range(2):
            pv = ps[banks[b]][:, 0:260].rearrange("p (h d) -> p h d", d=65)
            if typ == 0:
                S.add("dve", lambda e, pv=pv, b=b: e.tensor_tensor(
                    out=den[:, 4 * b:4 * b + 4].unsqueeze(2), in0=pv[:, :, 64:65],
                    in1=esink[:, l * 8 + 4 * b:l * 8 + 4 * b + 4].unsqueeze(2), op=ALU.add),
                    r=[*pn(banks[b]), "c:esink"], w=[("den", typ, b)])
            else:
                S.add("dve", lambda e, pv=pv, b=b: e.tensor_scalar(
                    out=den[:, 4 * b:4 * b + 4].unsqueeze(2), in0=pv[:, :, 64:65], scalar1=1e-30, scalar2=None, op0=ALU.max),
                    r=[*pn(banks[b])], w=[("den", typ, b)])
        S.add("dve", lambda e: e.reciprocal(out=rden, in_=den), r=[("den", typ, 0), ("den", typ, 1)], w=[("rden", typ)])
        for b in range(2):
            pv = ps[banks[b]][:, 0:260].rearrange("p (h d) -> p h d", d=65)
            S.add("dve", lambda e, pv=pv, b=b, i=i: e.tensor_tensor(
                out=On[i][:, 256 * b:256 * b + 256].rearrange("p (h d) -> p h d", d=64), in0=pv[:, :, 0:64],
                in1=rden[:, 4 * b:4 * b + 4].unsqueeze(2).to_broadcast([128, 4, 64]), op=ALU.mult),
                r=[*pn(banks[b]), ("rden", typ)], w=[("On", i, b)])
        ssq = small[:, 32 + 4 * typ:32 + 4 * typ + 1]
        vv = small[:, 32 + 4 * typ + 1:32 + 4 * typ + 2]
        rs = small[:, 32 + 4 * typ + 2:32 + 4 * typ + 3]
        def s1():
            S.tag = tg
            S.add("act", lambda e, i=i: e.activation(out=sq[:, 0, :], in_=On[i][:], func=AF.Square, accum_out=ssq),
                  r=[("On", i, 0), ("On", i, 1)], w=[("ssq", typ), ("sq", 0)])

        def s2():
            S.tag = tg
            S.add("dve", lambda e: e.tensor_scalar(out=vv, in0=ssq, scalar1=1.0 / 512, scalar2=EPS, op0=ALU.mult, op1=ALU.add),
                  r=[("ssq", typ)], w=[("vv", typ)])

        def s3():
            S.tag = tg
            S.add("pool", lambda e: e.tensor_tensor(out=rs, in0=vv, in1=nhalf[:, 0:1], op=ALU.pow),
                  r=[("vv", typ), "c:nh"], w=[("rs", typ)])

        def s4():
            S.tag = tg
            S.add("act", lambda e, i=i: e.activation(out=Obf[:, obase:obase + 512], in_=On[i][:], func=AF.Copy, scale=rs),
                  r=[("On", i, 0), ("On", i, 1), ("rs", typ)], w=[("Obf", typ)])
        return [(1, s1), (2, s2), (3, s3), (4, s4)]


    BP = os.environ.get("K_BP", "wWnNt")
    NAP = os.environ.get("K_NAP", "emv")
    NAENG = os.environ.get("K_NAENG", "dve")

    def phase_B(l, t, last, pre=None, tail_hook=None):
        t0 = 256 * (l + 1) + TT * t
        if pre is not None:
            pre()
        S.add("sp", lambda e: e.dma_start(out=hB[:], in_=tok_ap(hs, t0)),
              r=hs_names(t0), w=[("hB", c) for c in range(KC)], dma="hB")
        if True:
            S.add("sp", lambda e: e.dma_start(out=pTt[:], in_=pbf[l * 256:(l + 1) * 256, t0:t0 + TT].rearrange("(c p) t -> p c t", p=128)),
                  r=[("pbf", l)], w=["pTt"], dma="pTt")
        items = []
        pres, blk_first = [], []
        psT = ps[7].bitcast(BF16)
        ggi = (l * 9 + N_GROUP) * 8

        def kv_loc(m):
            a = m - 2 * l
            return (a // 4) % 2, a % 4

        for jb in range(4):
            n = 2 * (l + 1) + 4 * t + jb
            qa = n - 2 * l
            qslot, qsub = (qa // 4) % 2, qa % 4
            qs = slice(qsub * 128, qsub * 128 + 128)
            zb = nxt("Z", NZ)

            def pre(qslot=qslot, qsub=qsub, qs=qs, zb=zb):
                S.tag = "B.qz"
                for h_ in range(2):
                    hp = slice(64 * h_, 64 * h_ + 64)
                    S.add("pool", lambda e, hp=hp, h_=h_: e.tensor_copy(
                        out=QZw[zb][hp, h_, :].rearrange("p (c q) -> p c q", c=4), in_=QW[qslot][hp, qsub, :, :]),
                        r=[("QT", qslot, c) for c in range(4)], w=[("QZw", zb)])
                    S.add("act", lambda e, hp=hp, h_=h_: e.activation(
                        out=QZn[zb][hp, :, h_, :], in_=QT[qslot][hp, 4:8, qs], func=AF.Copy),
                        r=[("QT", qslot, c) for c in range(4, 8)], w=[("QZn", zb)])
            pres.append(pre)
            blk_first.append(len(items))
            for g in range(2):
                hp = slice(64 * g, 64 * g + 64)
                for rel in range(3):
                    m = n - 1 + rel
                    ks, ksub = kv_loc(m)
                    it = {}
                    st = {}

                    def S_(it=it, st=st, ks=ks, ksub=ksub, g=g, zb=zb):
                        S.tag = "B.win"
                        pS = 4 + nxt("SB", NSB)
                        st["pS"] = pS
                        S.add("pe", lambda e: e.matmul(
                            ps[pS], KT[ks][:, 0, ksub * 128:ksub * 128 + 128], QZw[zb][:, g, :], start=True, stop=True),
                            r=[("KT", ks, 0), ("QZw", zb)], w=pn(pS))

                    def SM_(st=st, m=m, rel=rel, g=g):
                        pS = st["pS"]
                        ei, pi = nxt("E", NE), nxt("P", NP_)
                        st["pi"] = pi
                        S.add("act", lambda e: e.activation(
                            out=Eb[ei][:], in_=ps[pS], func=AF.Exp, scale=HD ** -0.5, bias=wkb[:, m:m + 1]),
                            r=pn(pS) + ["c:wkb"], w=[("E", ei)])
                        tb = wt_tab[:, (rel * 8 + 4 * g) * 128:(rel * 8 + 4 * g + 4) * 128]
                        S.add("dve", lambda e: e.tensor_tensor(out=Pb[pi][:], in0=Eb[ei][:], in1=tb, op=ALU.mult),
                              r=[("E", ei), "c:wtab"], w=[("P", pi)])

                    def PV_(st=st, ks=ks, ksub=ksub, g=g, rel=rel):
                        S.tag = "B.winpv"
                        pi = st["pi"]
                        for c in range(4):
                            S.add("pe", lambda e, c=c: e.matmul(
                                ps[g][:, c * 65:(c + 1) * 65], Pb[pi][:, c * 128:(c + 1) * 128], V[ks][:, ksub, g, :],
                                start=(rel == 0 and c == 0), stop=(rel == 2), skip_group_check=True),
                                r=[("P", pi), ("V", ks, ksub, 2)], w=pn(g))
                    it.update(S=S_, SM=SM_, PV=PV_, post=None)
                    items.append(it)

            def postw(l=l):
                S.tag = "B.winfin"
                return attn_finish(l, 0, (0, 1), 0)
            items[-1]["post"] = postw
            offs = NA_OFFS[n]
            for hg in range(2):
                for oi_k, o in enumerate(offs):
                    m = n + o
                    ks, ksub = kv_loc(m)
                    it = {}
                    st = {}

                    def S_(st=st, ks=ks, ksub=ksub, hg=hg, zb=zb):
                        S.tag = "B.na"
                        pS = 4 + nxt("SB", NSB)
                        st["pS"] = pS
                        for pr in range(2):
                            j = 2 * hg + pr
                            S.add("pe", lambda e, pr=pr, j=j: e.matmul(
                                ps[pS][:, pr * 256:(pr + 1) * 256], KT[ks][:, 1 + j, ksub * 128:ksub * 128 + 128],
                                QZn[zb][:, j, :, :].rearrange("p h q -> p (h q)"), start=True, stop=True),
                                r=[("KT", ks, 1 + j), ("QZn", zb)], w=pn(pS))

                    def SM_(st=st, n=n, o=o, hg=hg):
                        pS = st["pS"]
                        ei, pi = nxt("E", NE), nxt("P", NP_)
                        st["pi"] = pi
                        if NA_SAMEI[n][o + 3]:
                            col = (n * NOFF + (o + 3)) * 2
                            S.add("act", lambda e, col=col: e.activation(
                                out=Eb[ei][:], in_=ps[pS], func=AF.Exp, scale=HD ** -0.5, bias=nab[:, col:col + 1]),
                                r=pn(pS) + ["c:nab"], w=[("E", ei)])
                        for i_ in (range(0) if NA_SAMEI[n][o + 3] else range(2)):
                            col = (n * NOFF + (o + 3)) * 2 + i_
                            srcp = ps[pS].rearrange("p (c i q) -> p c i q", c=4, i=2)[:, :, i_, :]
                            dst = Eb[ei][:].rearrange("p (c i q) -> p c i q", c=4, i=2)[:, :, i_, :]
                            S.add("act", lambda e, srcp=srcp, dst=dst, col=col: e.activation(
                                out=dst, in_=srcp, func=AF.Exp, scale=HD ** -0.5, bias=nab[:, col:col + 1]),
                                r=pn(pS) + ["c:nab"], w=[("E", ei)])
                        gb = Gt[:, ((o + 3) * 8 + 4 * hg) * 128:((o + 3) * 8 + 4 * hg + 4) * 128]
                        S.add("dve", lambda e: e.tensor_tensor(out=Pb[pi][:], in0=Eb[ei][:], in1=gb, op=ALU.mult),
                              r=[("E", ei)] + [("G", o + 3, a_, b_) for a_ in range(2) for b_ in range(2)], w=[("P", pi)])

                    def PV_(st=st, ks=ks, ksub=ksub, hg=hg, oi_k=oi_k, nof=len(offs)):
                        S.tag = "B.napv"
                        pi = st["pi"]
                        for c in range(4):
                            h = 4 * hg + c
                            S.add("pe", lambda e, c=c, h=h: e.matmul(
                                ps[2 + hg][:, c * 65:(c + 1) * 65], Pb[pi][:, c * 128:(c + 1) * 128], V[ks][:, ksub, 2 + h, :],
                                start=(oi_k == 0 and c == 0), stop=(oi_k == nof - 1), skip_group_check=True),
                                r=[("P", pi), ("V", ks, ksub, hg)], w=pn(2 + hg))
                    it.update(S=S_, SM=SM_, PV=PV_, post=None)
                    items.append(it)

            def postn(l=l, jb=jb):
                S.tag = "B.nafin"
                stages = attn_finish(l, 1, (2, 3), 512)

                def tr(jb=jb):
                    S.tag = "B.tr"
                    for c in range(KC):
                        S.add("pe", lambda e, c=c: e.transpose(psT[:, c * 128:(c + 1) * 128], Obf[:, c * 128:(c + 1) * 128], identb[:]),
                              r=[("Obf", c // 4), "c:identb"], w=pn(7))
                    S.add("dve", lambda e: e.tensor_tensor(
                        out=xn[:, :, jb * 128:(jb + 1) * 128], in0=psT.rearrange("p (c t) -> p c t", c=8),
                        in1=gn[:, ggi:ggi + 8].unsqueeze(2).to_broadcast([128, 8, 128]), op=ALU.mult),
                        r=pn(7) + ["c:gn"], w=[("xn", c) for c in range(KC)])
                return stages + [(TRDELAY, tr)]
            items[-1]["post"] = postn
        LA = int(os.environ.get("K_LA", "3"))
        deferred = []
        pres[0]()
        for k in range(len(items) + LA):
            if k < len(items):
                if k in blk_first:
                    bi = blk_first.index(k)
                    if bi + 1 < len(pres) and NZ > 1:
                        pres[bi + 1]()
                    elif NZ == 1 and bi > 0:
                        pres[bi]()
                items[k]["S"]()
                items[k]["SM"]()
            j = k - LA
            if j >= 0:
                items[j]["PV"]()
                for d in deferred:
                    d[0] -= 1
                for d in [d for d in deferred if d[0] <= 0]:
                    d[1]()
                    deferred.remove(d)
                if items[j]["post"] is not None:
                    for dly, fn in items[j]["post"]():
                        deferred.append([dly, fn])
        for d in sorted(deferred, key=lambda d: d[0]):
            d[1]()
        cv_hook(2)
        S.tag = "B.wo"
        for dc in range(KC):
            wsl, wn = use_slab("wo", dc)
            pd_ = 4 + dc % 2
            for kc in range(KC):
                S.add("pe", lambda e, kc=kc, wsl=wsl, pd_=pd_: e.matmul(
                    ps[pd_][:], wsl[:, kc * 128:(kc + 1) * 128], xn[:, kc, :], start=(kc == 0), stop=(kc == KC - 1)),
                    r=[wn, ("xn", kc)], w=[*pn(pd_)])
            if dc >= 1 and ILSTAT:
                S.tag = "B.wo_post/stats"
                stat_mm(dc - 1)
                S.tag = "B.wo"
            S.add("act", lambda e, dc=dc, pd_=pd_: e.activation(out=fo[:, dc, :], in_=ps[pd_][:], func=AF.Copy, scale=gcol(l, N_MIX_POST, dc)),
                  r=[*pn(pd_), "c:gn"], w=[("fo", dc)])
            S.add("act", lambda e, dc=dc, pd_=pd_: e.activation(out=sq[:, dc, :], in_=ps[pd_][:], func=AF.Square),
                  r=[*pn(pd_)], w=[("sq", dc)])
        S.tag = "B.wo_post"
        postnorm_update(hB, "hB", l, N_MIX_POST, False, done=(KC - 1 if ILSTAT else 0))
        ffn(hB, "hB", l, N_FFN2_PRE, N_FFN2_POST)
        cv_hook()
        S.tag = "B.ple_pre"
        prenorm(hB, "hB", l, N_PLE_PRE)
        S.tag = "B.ple"
        for fc in range(KC):
            wsl, wn = use_slab("pg", fc)
            pa, pb = fc % 2, 2 + fc % 2
            for kc in range(KC):
                S.add("pe", lambda e, kc=kc, wsl=wsl, pa=pa: e.matmul(
                    ps[pa][:], wsl[:, kc * 128:(kc + 1) * 128], xn[:, kc, :], start=(kc == 0), stop=(kc == KC - 1)),
                    r=[wn, ("xn", kc)], w=[*pn(pa)])
            for kc in range(2):
                S.add("pe", lambda e, kc=kc, wsl=wsl, pb=pb: e.matmul(
                    ps[pb][:], wsl[:, (8 + kc) * 128:(9 + kc) * 128], pTt[:, kc, :], start=(kc == 0), stop=(kc == 1)),
                    r=[wn, "pTt"], w=[*pn(pb)])
            if fc >= 1 and ILSTAT:
                S.tag = "B.ple_post/stats"
                stat_mm(fc - 1)
                S.tag = "B.ple"
            j = nxt("sg", 2)
            S.add("act", lambda e, j=j, pa=pa: e.activation(out=sg[j][:], in_=ps[pa][:], func=AF.Sigmoid),
                  r=[*pn(pa)], w=[("sg", j)])
            S.add("dve", lambda e, j=j, pb=pb, fc=fc: e.tensor_tensor(out=fo[:, fc, :], in0=ps[pb][:], in1=sg[j][:], op=ALU.mult),
                  r=[*pn(pb), ("sg", j)], w=[("fo", fc)])
            S.add("act", lambda e, fc=fc: e.activation(out=sq[:, fc, :], in_=fo[:, fc, :], func=AF.Square),
                  r=[("fo", fc)], w=[("sq", fc)])
            S.add("act", lambda e, fc=fc: e.activation(out=fo[:, fc, :], in_=fo[:, fc, :], func=AF.Copy, scale=gcol(l, N_PLE_POST, fc)),
                  r=[("fo", fc), "c:gn"], w=[("fo", fc)])
        S.tag = "B.ple_post"
        postnorm_update(hB, "hB", l, N_PLE_POST, False, hook=tail_hook, done=(KC - 1 if ILSTAT else 0))
        def store_B():
            tg = S.tag
            S.tag = "st"
            if last:
                y0 = t0 - HALO
                S.add("sp", lambda e: e.dma_start(out=tok_ap(yT, y0), in_=hB[:]),
                      r=[("hB", c) for c in range(KC)], w=[("yT", t)], dma="stB")
            else:
                S.add("sp", lambda e: e.dma_start(out=tok_ap(hs, t0), in_=hB[:]),
                      r=[("hB", c) for c in range(KC)], w=hs_names(t0), dma="stB")
            S.tag = tg
        return store_B

    cur_l = -1
    preloaded = set()
    prenormed = set()
    pending_store = None
    DEFERST = os.environ.get("K_DEFERST", "1") == "1"
    EARLYPRE = os.environ.get("K_EARLYPRE", "1") == "1"
    for idx, (ph, l, t) in enumerate(order):
        if l != cur_l:
            build_G(l)
            cur_l = l
        if ph == "A":
            phase_A(l, t, preloaded=((l, t) in preloaded), prenormed=((l, t) in prenormed and EARLYPRE), mid_hook=pending_store)
            pending_store = None
        else:
            pre = None
            hook = None
            if idx + 1 < len(order) and order[idx + 1][0] == "A":
                _, l2, t2 = order[idx + 1]
                preloaded.add((l2, t2))
                pre = (lambda l2=l2, t2=t2: load_hA(l2, t2))
                if EARLYPRE:
                    prenormed.add((l2, t2))

                    def hook(l2=l2):
                        S.tag = "hA.ffn_pre"
                        prenorm(hA, "hA", l2, N_FFN1_PRE)
            st_fn = phase_B(l, t, (l == NL - 1), pre, hook)
            if idx + 1 < len(order) and order[idx + 1][0] == "A" and DEFERST:
                pending_store = st_fn
            else:
                st_fn()
    if stage == "G":
        build_G(0)
    full = (n_layers == NL and stage is None)
    outs = [("yT", t) for t in range(8)] if full else []
    S.add("sp", lambda e: e.nop(), r=outs + [("hs", k) for k in range(NLOC // 256)], final=True)
    if os.environ.get("K_DUMPTAGS"):
        with open(os.environ["K_DUMPTAGS"], "w") as f:
            for op in S.ops:
                if op["eng"] == "pe":
                    f.write(op["tag"] + "\n")
    S.emit(es)
    es.close()
    return nc


def _slab(W):
    K, Fd = W.shape
    return np.ascontiguousarray(W.reshape(K // 128, 128, Fd // 128, 128).transpose(2, 1, 0, 3))


_QA, _KA, _VA, _QB, _KB, _VB = 0, 512, 640, 768, 1280, 1792


def _fm_cols():
    cols = []
    for c in range(4):
        cols += list(range(_QA + c * 64, _QA + c * 64 + 64)) + list(range(_QA + (4 + c) * 64, _QA + (4 + c) * 64 + 64))
    cols += list(range(_KA, _KA + 128))
    cols += list(range(_QB, _QB + 512))
    cols += list(range(_KB, _KB + 512))
    return np.array(cols)


def _pack_weights(inp):
    out = np.empty((NL, 2, GRP), np.float32)
    fmc = _fm_cols()
    for l in range(NL):
        parts = []
        sg_, su_ = _slab(inp["w_ffn1_gate"][l]), _slab(inp["w_ffn1_up"][l])
        for fc in range(FC):
            parts.append(np.stack([sg_[fc], su_[fc]], axis=1).reshape(128, -1))
        sd_ = _slab(inp["w_ffn1_down"][l])
        for dc in range(8):
            for hf in range(2):
                parts.append(sd_[dc][:, hf * 11:(hf + 1) * 11, :].reshape(128, -1))
        sf_ = _slab(inp["w_in"][l][:, fmc])
        for si in range(13):
            parts.append(sf_[si].reshape(128, -1))
        wv = inp["w_in"][l]
        for cols in (slice(_VB, _VB + 256), slice(_VB + 256, _VB + 512), slice(_VA, _VA + 128)):
            m = wv[:, cols]
            parts.append(np.ascontiguousarray(m.reshape(8, 128, -1).transpose(1, 0, 2)).reshape(128, -1))
        out[l, 0] = np.concatenate([p.reshape(-1) for p in parts])
        parts = []
        so_ = _slab(inp["w_out"][l])
        for dc in range(8):
            parts.append(so_[dc].reshape(128, -1))
        sg_, su_ = _slab(inp["w_ffn2_gate"][l]), _slab(inp["w_ffn2_up"][l])
        for fc in range(FC):
            parts.append(np.stack([sg_[fc], su_[fc]], axis=1).reshape(128, -1))
        sd_ = _slab(inp["w_ffn2_down"][l])
        for dc in range(8):
            for hf in range(2):
                parts.append(sd_[dc][:, hf * 11:(hf + 1) * 11, :].reshape(128, -1))
        spg, spp = _slab(inp["w_ple_gate"][l]), _slab(inp["w_ple_proj"][l])
        for fc in range(8):
            parts.append(np.concatenate([spg[fc], spp[fc]], axis=1).reshape(128, -1))
        out[l, 1] = np.concatenate([p.reshape(-1) for p in parts])
    return out.reshape(-1)


CHUNKS = [("prompt", 0, 0, 8192), ("prompt", 0, 4096, 8192), ("prompt", 1, 0, 8192), ("prompt", 1, 4096, 8192),
          ("sample", 0, 0, 16384), ("sample", 0, 4096, 16384), ("sample", 0, 8192, 16384), ("sample", 0, 12288, 16384)]


def _na_valid_rows(r, rows):
    if 0 <= r < rows:
        s = min(max(r - 4, 0), rows - 8)
    else:
        s = r - 4
    return s, s + 8


def _core_tables(a, Lseq):
    rows = Lseq // 64
    nab = np.full((128, NPAIR, NOFF, 2), NEG, np.float32)
    valid_any = np.zeros((NPAIR, NOFF), bool)
    for n in range(NPAIR):
        for i_ in range(2):
            r = (a - HALO) // 64 + 2 * n + i_
            s, e_ = _na_valid_rows(r, rows)
            for oi in range(NOFF):
                for j in range(2):
                    kr = (a - HALO) // 64 + 2 * (n + oi - 3) + j
                    ok = s <= kr < e_
                    if 0 <= r < rows and not (0 <= kr < rows):
                        ok = False
                    if ok:
                        nab[64 * j:64 * j + 64, n, oi, i_] = 0.0
                        valid_any[n, oi] = True
    wkb = np.full((128, NPAIR), NEG, np.float32)
    for b in range(NPAIR):
        g = a - HALO + 128 * b
        if 0 <= g < Lseq:
            wkb[:, b] = 0.0
    return nab.reshape(128, -1), wkb, valid_any


def _na_offsets():
    va = np.zeros((NPAIR, NOFF), bool)
    same = np.ones((NPAIR, NOFF), bool)
    for (_, _, a, Ls) in CHUNKS:
        nb_, _, v_ = _core_tables(a, Ls)
        va |= v_
        nb_ = nb_.reshape(128, NPAIR, NOFF, 2)
        same &= (nb_[:, :, :, 0] == nb_[:, :, :, 1]).all(axis=0)
    global NA_SAMEI
    NA_SAMEI = same
    offs = []
    for n in range(NPAIR):
        o = [oi - 3 for oi in range(NOFF) if va[n, oi] and 0 <= n + oi - 3 < NPAIR]
        offs.append(o)
    return offs


NA_SAMEI = None
NA_OFFS = _na_offsets()


def _const_tables():
    slopes = np.exp2(-(8.0 / 8) * np.arange(1, 9, dtype=np.float32)).astype(np.float32)
    j = np.arange(128)[:, None, None, None]
    rel = np.arange(3)[None, :, None, None]
    i_ = np.arange(128)[None, None, None, :]
    dist = np.abs((rel - 1) * 128 + j - i_).astype(np.float32)
    tab = np.exp(-slopes[None, None, :, None] * dist) * (dist <= 128)
    wtab = tab.astype(np.float32).reshape(128, -1)
    c = np.arange(64)
    cs = np.clip(c - 8, 0, 48)
    col_ok = (c[None, :] >= cs[:, None]) & (c[None, :] < cs[:, None] + 16)
    cm = col_ok.T.astype(np.float32)
    cmask = np.concatenate([cm, cm], axis=0)
    return wtab, cmask


_PROG = {}


def _prep_inputs(inp):
    inp = {k: np.asarray(v) for k, v in inp.items()}
    wall = _pack_weights(inp)
    gn = np.ascontiguousarray(inp["norm_g"].reshape(NL, 9, 8, 128).transpose(3, 0, 1, 2)).reshape(128, -1)
    sinkb = np.ascontiguousarray(np.broadcast_to(inp["sink"].reshape(1, -1), (128, NL * 8))).astype(np.float32)
    rp = np.zeros((NL, 8, 15, 127), np.float32)
    rp[..., 48:79] = inp["rpb"]
    rp = rp.reshape(NL * 120, 127)
    rpbp = np.concatenate([np.zeros((NL * 120, 64), np.float32), rp], axis=1)
    wtab, cmask = _const_tables()
    identf = np.eye(128, dtype=np.float32)
    shared = dict(wall=wall, gn=gn.astype(np.float32), sinkb=sinkb, rpbp=rpbp, wtab=wtab, cmask=cmask, identf=identf)
    maps = []
    for (kind, b, a, Ls) in CHUNKS:
        x = inp["x_" + kind][b]
        p = inp["p_" + kind][:, b]
        lo, hi = a - HALO, a + NTOK + HALO
        slo, shi = max(lo, 0), min(hi, Ls)
        xl = np.zeros((NLOC, D), np.float32)
        xl[slo - lo:shi - lo] = x[slo:shi]
        pl = np.zeros((NL, NLOC, 256), np.float32)
        pl[:, slo - lo:shi - lo] = p[:, slo:shi]
        nab, wkb, _ = _core_tables(a, Ls)
        m = dict(shared)
        m["xT"] = np.ascontiguousarray(xl.T)
        m["pT"] = np.ascontiguousarray(pl.transpose(0, 2, 1)).reshape(NL * 256, NLOC)
        m["nab"] = nab
        m["wkb"] = wkb
        maps.append(m)
    return maps


def kernel(**inputs):
    maps = _prep_inputs(inputs)
    if "nc" not in _PROG:
        _PROG["nc"] = build_program()
    res = run_bass_kernel_spmd(_PROG["nc"], maps, core_ids=list(range(8)))
    y_prompt = np.empty((2, 8192, D), np.float32)
    y_sample = np.empty((1, 16384, D), np.float32)
    for ci, (kind, b, a, Ls) in enumerate(CHUNKS):
        y = np.asarray(res.results[ci]["yT"]).T
        (y_prompt if kind == "prompt" else y_sample)[b, a:a + NTOK] = y
    return (y_prompt, y_sample)
```

```python
import contextlib
import os

import numpy as np

import concourse.bass as bass
import concourse.mybir as mybir
from concourse.bass_utils import run_bass_kernel_spmd

F32 = mybir.dt.float32
BF16 = mybir.dt.bfloat16
AF = mybir.ActivationFunctionType
ALU = mybir.AluOpType
AX = mybir.AxisListType

D = 1024
DFF = 2816
NL = 4
HD = 64
NTOK = 4096
HALO = 1024
NLOC = NTOK + 2 * HALO
TT = 512
KC = 8
FC = 22
EPS = 1e-6
NEG = -30000.0
NPAIR = NLOC // 128
NOFF = 7
(N_FFN1_PRE, N_FFN1_POST, N_MIX_PRE, N_MIX_POST, N_FFN2_PRE, N_FFN2_POST,
 N_PLE_PRE, N_PLE_POST, N_GROUP) = range(9)

SAME_ENGINE_SYNC = os.environ.get("K_SAMESYNC", "1") == "1"
RING = 5
SLOT_E = 2048

E_GU, E_D, E_FM, E_WO, E_PG = 2048, 1408, 1024, 1024, 1280
A_SLABS = []
_off = 0
for _i in range(FC):
    A_SLABS.append(("gu", _i, _off, E_GU, 1)); _off += 128 * E_GU
for _i in range(16):
    A_SLABS.append(("d", _i, _off, E_D, 1)); _off += 128 * E_D
for _i in range(0, 12, 2):
    A_SLABS.append(("fm", _i, _off, 2 * E_FM, 2)); _off += 2 * 128 * E_FM
A_SLABS.append(("fm", 12, _off, E_FM, 1)); _off += 128 * E_FM
for _i, _e in enumerate((2048, 2048, 1024)):
    A_SLABS.append(("wv", _i, _off, _e, 1)); _off += 128 * _e
A_SIZE = _off
B_SLABS = []
_off = 0
for _i in range(0, 8, 2):
    B_SLABS.append(("wo", _i, _off, 2 * E_WO, 2)); _off += 2 * 128 * E_WO
for _i in range(FC):
    B_SLABS.append(("gu", _i, _off, E_GU, 1)); _off += 128 * E_GU
for _i in range(16):
    B_SLABS.append(("d", _i, _off, E_D, 1)); _off += 128 * E_D
for _i in range(8):
    B_SLABS.append(("pg", _i, _off, E_PG, 1)); _off += 128 * E_PG
B_SIZE = _off
assert A_SIZE == B_SIZE
GRP = A_SIZE
WTOT = NL * 2 * GRP
CV_PIECE = 128 * 2688
assert GRP % CV_PIECE == 0


class Sched:
    def __init__(self, nc):
        self.nc = nc
        self.ops = []
        self.tag = ""

    def add(self, eng, fn, r=(), w=(), dma=None, group=False, final=False):
        self.ops.append(dict(eng=eng, fn=fn, r=tuple(r), w=tuple(w), dma=dma, group=group, ms=False, final=final, tag=self.tag))

    def emit(self, es):
        nc = self.nc
        engs = {"pe": nc.tensor, "act": nc.scalar, "dve": nc.vector, "pool": nc.gpsimd, "sp": nc.sync}
        ops = self.ops
        last_w, readers, last_dma, last_any_dma = {}, {}, {}, {}
        known = {e: {} for e in engs}
        known_dma = {e: set() for e in engs}
        for i, op in enumerate(ops):
            deps = set()
            for b in op["r"]:
                if b in last_w:
                    deps.add(last_w[b])
            for b in op["w"]:
                if b in last_w:
                    deps.add(last_w[b])
                deps.update(readers.get(b, ()))
            if op["final"]:
                deps.update(last_any_dma.values())
            if op["dma"]:
                last_any_dma[op["dma"]] = i
            if op["dma"] and not op["group"]:
                p = last_dma.get(op["dma"])
                if p is not None:
                    deps.add(p)
                last_dma[op["dma"]] = i
            for b in op["r"]:
                if not (isinstance(b, str) and b.startswith("c:")):
                    readers.setdefault(b, []).append(i)
            for b in op["w"]:
                last_w[b] = i
                readers[b] = []
            e = op["eng"]
            waits, best = [], {}
            for d in deps:
                pd = ops[d]
                if pd["dma"]:
                    if d not in known_dma[e]:
                        known_dma[e].add(d)
                        waits.append(d)
                else:
                    pe_ = pd["eng"]
                    if pe_ == e and not op["dma"] and (e == "pe" or not SAME_ENGINE_SYNC):
                        continue
                    if d > best.get(pe_, -1):
                        best[pe_] = d
            for pe_, d in best.items():
                if known[e].get(pe_, -1) >= d:
                    continue
                known[e][pe_] = d
                waits.append(d)
            op["waits"] = waits
            for d in waits:
                ops[d]["ms"] = True
        gtot = {}
        for op in ops:
            if op["dma"] and op["group"]:
                gtot[op["dma"]] = gtot.get(op["dma"], 0) + 16
        sems = {e: es.enter_context(nc.semaphore("s_" + e)) for e in engs}
        cnt = {e: 0 for e in engs}
        dsem, dcnt, token = {}, {}, {}
        for i, op in enumerate(ops):
            e = engs[op["eng"]]
            for d in op["waits"]:
                s, v = token[d]
                e.wait_ge(s, v)
            inst = op["fn"](e)
            if op["dma"]:
                name = op["dma"]
                if name not in dsem:
                    dsem[name] = es.enter_context(nc.semaphore("d_" + name))
                    dcnt[name] = 0
                dcnt[name] += 16
                inst.then_inc(dsem[name], 16)
                token[i] = (dsem[name], gtot[name] if op["group"] else dcnt[name])
            elif op["ms"]:
                cnt[op["eng"]] += 1
                inst.then_inc(sems[op["eng"]], 1)
                token[i] = (sems[op["eng"]], cnt[op["eng"]])


def build_program(n_layers=NL, stage=None, debug=False):
    nc = bass.Bass("TRN2", target_bir_lowering=False)
    es = contextlib.ExitStack()
    S = Sched(nc)

    def dram(name, shape, dt, kind):
        return nc.dram_tensor(name, shape, dt, kind=kind).ap()

    xT = dram("xT", [D, NLOC], F32, "ExternalInput")
    pT = dram("pT", [NL * 256, NLOC], F32, "ExternalInput")
    wall = dram("wall", [WTOT], F32, "ExternalInput")
    gn_d = dram("gn", [128, NL * 9 * 8], F32, "ExternalInput")
    sink_d = dram("sinkb", [128, NL * 8], F32, "ExternalInput")
    rpb_d = dram("rpbp", [NL * 120, 191], F32, "ExternalInput")
    wtab_d = dram("wtab", [128, 3 * 8 * 128], F32, "ExternalInput")
    cmask_d = dram("cmask", [128, 64], F32, "ExternalInput")
    ident_d = dram("identf", [128, 128], F32, "ExternalInput")
    nab_d = dram("nab", [128, NPAIR * NOFF * 2], F32, "ExternalInput")
    wkb_d = dram("wkb", [128, NPAIR], F32, "ExternalInput")
    yT = dram("yT", [D, NTOK], F32, "ExternalOutput")
    wbf = dram("wbf", [WTOT], BF16, "Internal")
    pbf = dram("pbf", [NL * 256, NLOC], BF16, "Internal")
    hs = dram("hs", [D, NLOC], F32, "ExternalOutput" if debug else "Internal")

    def sb(name, shape, dt):
        return es.enter_context(nc.sbuf_tensor(name, shape, dt))

    hA = sb("hA", [128, KC, TT], F32)
    hB = sb("hB", [128, KC, TT], F32)
    xn = sb("xn", [128, KC, TT], BF16)
    sq = sb("sq", [128, KC, TT], BF16)
    act = sb("act", [128, FC, TT], BF16)
    fo = sb("fo", [128, KC, TT], F32)
    ssv = [sb(f"ssv{i}", [128, TT], F32) for i in range(2)]
    rstd = [sb(f"rstd{i}", [128, TT], F32) for i in range(2)]
    sg = [sb(f"sg{i}", [128, TT], F32) for i in range(2)]
    QT = [sb(f"QT{i}", [128, 8, TT], BF16) for i in range(2)]
    QW = [QT[i][:, 0:4, :].rearrange("p a t -> p (a t)").rearrange("p (s c q) -> p s c q", s=4, c=4) for i in range(2)]
    KT = [sb(f"KT{i}", [128, 5, TT], BF16) for i in range(2)]
    V = [sb(f"V{i}", [128, 4, 10, 65], BF16) for i in range(2)]
    NE = 3
    NP_ = int(os.environ.get("K_NP", "4"))
    NSB = int(os.environ.get("K_NSB", "4"))
    NZ = int(os.environ.get("K_NZ", "2"))
    QZw = [sb(f"QZw{i}", [128, 2, 512], BF16) for i in range(NZ)]
    QZn = [sb(f"QZn{i}", [128, 4, 2, 128], BF16) for i in range(NZ)]
    Eb = [sb(f"E{i}", [128, 512], F32) for i in range(NE)]
    Pb = [sb(f"P{i}", [128, 512], BF16) for i in range(NP_)]
    On = [sb(f"On{i}", [128, 512], F32) for i in range(2)]
    Obf = sb("Obf", [128, 1024], BF16)
    small = sb("small", [128, 64], F32)
    wt_tab = sb("wt_tab", [128, 3 * 8 * 128], BF16)
    Gt = sb("Gt", [128, NOFF * 8 * 128], BF16)
    pTt = sb("pTt", [128, 2, TT], BF16)
    ws = [sb(f"ws{i}", [128, SLOT_E], BF16) for i in range(RING)]
    gn = sb("gn_s", [128, NL * 9 * 8], F32)
    g5 = sb("g5_s", [128, NL * 9 * 8], F32)
    esink = sb("esink", [128, NL * 8], F32)
    cmask = sb("cmask_s", [128, 64], F32)
    identf = sb("identf_s", [128, 128], F32)
    identb = sb("identb_s", [128, 128], BF16)
    ones = sb("ones_s", [128, 128], BF16)
    nhalf = sb("nhalf_s", [128, 1], F32)
    epst = sb("eps_s", [128, 1], F32)
    nab = sb("nab_s", [128, NPAIR * NOFF * 2], F32)
    wkb = sb("wkb_s", [128, NPAIR], F32)
    R2 = sb("R2", [120, 191], F32)
    _ps03 = [es.enter_context(nc.psum_tensor("ps%d" % i, [128, 512], F32)) for i in range(4)]
    psS = es.enter_context(nc.psum_tensor("psS", [128, 3, 512], F32))
    _ps7 = es.enter_context(nc.psum_tensor("ps7", [128, 512], F32))
    ps = [_ps03[0][:], _ps03[1][:], _ps03[2][:], _ps03[3][:], psS[:, 0, :], psS[:, 1, :], psS[:, 2, :], _ps7[:]]
    psH = psS[:].rearrange("p b (h x) -> p (b h) x", h=2)

    def pn(b):
        return [("psh", b, 0), ("psh", b, 1)] if 4 <= b <= 6 else ["ps%d" % b]

    def gcol(l, n, c):
        i = (l * 9 + n) * 8 + c
        return gn[:, i:i + 1]

    def g5col(l, n, c):
        i = (l * 9 + n) * 8 + c
        return g5[:, i:i + 1]

    PRO = os.environ.get("K_PRO", "abcdefgh")
    S.add("pool", lambda e: e.memset(ones[:], 1.0), w=["c:ones"])
    S.add("pool", lambda e: e.memset(nhalf[:], -0.5), w=["c:nh"])
    S.add("pool", lambda e: e.memset(epst[:], EPS), w=["c:eps"])
    for z_ in range(NZ):
        S.add("pool", lambda e, z_=z_: e.memset(QZw[z_][:], 0.0), w=[("QZw", z_)])
        S.add("pool", lambda e, z_=z_: e.memset(QZn[z_][:], 0.0), w=[("QZn", z_)])
    for s_ in (range(2) if "a" in PRO else []):
        S.add("pool", lambda e, s_=s_: e.memset(V[s_][:, :, :, 64:65], 1.0),
              w=[("V", s_, sub, g) for sub in range(4) for g in range(3)])
    S.add("sp", lambda e: e.dma_start(out=gn[:], in_=gn_d[:, :]), w=["c:gn"], dma="c0")
    S.add("sp", lambda e: e.dma_start(out=esink[:], in_=sink_d[:, :]), w=["esink_raw"], dma="c1")
    S.add("sp", lambda e: e.dma_start(out=cmask[:], in_=cmask_d[:, :]), w=["c:cmask"], dma="c2")
    S.add("sp", lambda e: e.dma_start(out=identf[:], in_=ident_d[:, :]), w=["c:identf"], dma="c3")
    if "b" in PRO:
        S.add("sp", lambda e: e.dma_start(out=nab[:], in_=nab_d[:, :]), w=["c:nab"], dma="c4")
        S.add("sp", lambda e: e.dma_start(out=wkb[:], in_=wkb_d[:, :]), w=["c:wkb"], dma="c5")
    if "c" in PRO:
        S.add("pool", lambda e: e.dma_start(out=identb[:], in_=ident_d[:, :]), w=["c:identb"], dma="c6")
        S.add("pool", lambda e: e.dma_start(out=wt_tab[:], in_=wtab_d[:, :]), w=["c:wtab"], dma="c7")
    if "d" in PRO:
        S.add("act", lambda e: e.activation(out=esink[:], in_=esink[:], func=AF.Exp), r=["esink_raw"], w=["c:esink"])
    if "e" in PRO:
        S.add("dve", lambda e: e.tensor_scalar(out=g5[:], in0=gn[:], scalar1=0.5, scalar2=None, op0=ALU.mult),
              r=["c:gn"], w=["c:g5"])
    cv_pending = []
    cv_done = set()
    cv_state = {"n": 0}

    def cv_piece(l, ph, k):
        o = (l * 2 + ph) * GRP + k * CV_PIECE
        st_ = f"cv{cv_state['n'] % 2}"
        cv_state["n"] += 1
        cv_done.add((l, ph, k))
        S.add("pool",
              lambda e: e.dma_start(
                  out=wbf[o:o + CV_PIECE].rearrange("(p x) -> p x", p=128),
                  in_=wall[o:o + CV_PIECE].rearrange("(p x) -> p x", p=128)),
              w=[("wbf", l, ph, k)], dma=st_)

    def cv_pbf(l):
        st_ = f"cv{cv_state['n'] % 2}"
        cv_state["n"] += 1
        S.add("pool", lambda e: e.dma_start(out=pbf[l * 256:(l + 1) * 256, :], in_=pT[l * 256:(l + 1) * 256, :]),
              w=[("pbf", l)], dma=st_)

    for l in range(n_layers):
        for ph in range(2):
            for k in range(GRP // CV_PIECE):
                if l == 0:
                    cv_piece(l, ph, k)
                else:
                    cv_pending.append(lambda l=l, ph=ph, k=k: cv_piece(l, ph, k))
            if ph == 0:
                if l == 0:
                    cv_pbf(l)
                else:
                    cv_pending.append(lambda l=l: cv_pbf(l))

    def cv_hook(nmax=1):
        tg = S.tag
        S.tag = "cv"
        for _ in range(nmax):
            if cv_pending:
                cv_pending.pop(0)()
        S.tag = tg

    order = []
    for l in range(n_layers):
        nA = 12 - l
        if stage in ("P", "G"):
            break
        if isinstance(stage, str) and stage.startswith("A0"):
            order += [("A", l, t) for t in range(nA if stage == "A0" else int(stage[2:]))]
            break
        order.append(("A", l, 0))
        for t in range(1, nA):
            order += [("A", l, t), ("B", l, t - 1)]
    seq = []
    for ph, l, t in order:
        tab = A_SLABS if ph == "A" else B_SLABS
        for si in range(len(tab)):
            seq.append((l, 0 if ph == "A" else 1, si))
    wstate = {"next_use": 0, "next_load": 0}

    def emit_load(n):
        if n >= len(seq):
            return
        l, ph, si = seq[n]
        tab = A_SLABS if ph == 0 else B_SLABS
        _, _, off, E, nsub = tab[si]
        o = (l * 2 + ph) * GRP + off
        slot = n % RING
        for k in range(off // CV_PIECE, (off + 128 * E - 1) // CV_PIECE + 1):
            assert (l, ph, k) in cv_done, ("weight piece not converted yet", l, ph, k)
        if nsub == 1:
            fn = (lambda e, o=o, E=E, slot=slot: e.dma_start(
                out=ws[slot][:, 0:E], in_=wbf[o:o + 128 * E].rearrange("(p x) -> p x", p=128)))
        else:
            fn = (lambda e, o=o, E=E, slot=slot, nsub=nsub: e.dma_start(
                out=ws[slot][:, 0:E].rearrange("p (s x) -> p s x", s=nsub),
                in_=wbf[o:o + 128 * E].rearrange("(s p x) -> p s x", s=nsub, p=128)))
        S.add("sp", fn,
              r=[("wbf", l, ph, k) for k in range(off // CV_PIECE, (off + 128 * E - 1) // CV_PIECE + 1)],
              w=[("ws", slot)], dma=f"w{slot}")

    def use_slab(kind, idx):
        cu = wstate.get("cur")
        if cu is not None and cu[0] == kind and cu[1] < idx < cu[1] + cu[2]:
            slot, esub = cu[3], cu[4]
            sub = idx - cu[1]
            return ws[slot][:, sub * esub:(sub + 1) * esub], ("ws", slot)
        n = wstate["next_use"]
        l, ph, si = seq[n]
        tab = A_SLABS if ph == 0 else B_SLABS
        assert tab[si][0] == kind and tab[si][1] == idx, (tab[si], kind, idx)
        if n == 0:
            for k in range(RING):
                emit_load(k)
        else:
            emit_load(n - 1 + RING)
        wstate["next_use"] = n + 1
        slot = n % RING
        E, nsub = tab[si][3], tab[si][4]
        wstate["cur"] = (kind, idx, nsub, slot, E // nsub)
        return ws[slot][:, 0:E // nsub], ("ws", slot)

    ILSTAT = os.environ.get("K_ILSTAT", "1") == "1"
    rot = {"ssv": 0, "sg": 0, "E": 0, "P": 0, "On": 0, "SB": 0, "SN": 0, "Z": 0}
    TRDELAY = int(os.environ.get("K_TRDELAY", "6"))
    NAMERGE = os.environ.get("K_NAMERGE", "1") == "1"

    def nxt(k, n):
        v = rot[k]
        rot[k] = (v + 1) % n
        return v

    def stat_mm(c):
        S.add("pe", lambda e, c=c: e.matmul(ps[6][:], ones[:], sq[:, c, :], start=(c == 0), stop=(c == KC - 1)),
              r=["c:ones", ("sq", c)], w=[*pn(6)])

    def stats_rstd(sq_names, done=0):
        S.tag = S.tag.split("/")[0] + "/stats"
        for c in range(done, KC):
            stat_mm(c)
        i = nxt("ssv", 2)
        S.add("act", lambda e, i=i: e.activation(out=ssv[i][:], in_=ps[6][:], func=AF.Sqrt, scale=1.0 / D, bias=epst[:, 0:1]),
              r=[*pn(6), "c:eps"], w=[("ssv", i)])
        S.add("dve", lambda e, i=i: e.reciprocal(out=rstd[i][:], in_=ssv[i][:]),
              r=[("ssv", i)], w=[("rstd", i)])
        return i

    def prenorm(hbuf, hname, l, nidx):
        for c in range(KC):
            S.add("act", lambda e, c=c: e.activation(out=sq[:, c, :], in_=hbuf[:, c, :], func=AF.Square),
                  r=[(hname, c)], w=[("sq", c)])
        i = stats_rstd([("sq", c) for c in range(KC)])
        for c in range(KC):
            S.add("dve", lambda e, c=c, i=i: e.scalar_tensor_tensor(
                out=xn[:, c, :], in0=hbuf[:, c, :], scalar=gcol(l, nidx, c), in1=rstd[i][:],
                op0=ALU.mult, op1=ALU.mult),
                r=[(hname, c), ("rstd", i), "c:gn"], w=[("xn", c)])

    def postnorm_update(hbuf, hname, l, nidx, half, hook=None, done=0):
        i = stats_rstd([("sq", c) for c in range(KC)], done)
        if hook is not None:
            tg = S.tag
            hook()
            S.tag = tg
        for c in range(KC):
            S.add("dve", lambda e, c=c, i=i: e.tensor_tensor(out=fo[:, c, :], in0=fo[:, c, :], in1=rstd[i][:], op=ALU.mult),
                  r=[("fo", c), ("rstd", i)], w=[("fo", c)])
        for c in range(KC):
            S.add("dve", lambda e, c=c: e.tensor_tensor(out=hbuf[:, c, :], in0=hbuf[:, c, :], in1=fo[:, c, :], op=ALU.add),
                  r=[("fo", c), (hname, c)], w=[(hname, c)])

    def ffn(hbuf, hname, l, n_pre, n_post, skip_pre=False, mid_hook=None):
        S.tag = hname + ".ffn_pre"
        if not skip_pre:
            prenorm(hbuf, hname, l, n_pre)
        S.tag = hname + ".ffn_gu"
        for fc in range(FC):
            if fc == 8 and mid_hook is not None:
                mid_hook()
            wsl, wn = use_slab("gu", fc)
            pg_, pu_ = fc % 2, 2 + fc % 2
            for kc in range(KC):
                S.add("pe", lambda e, kc=kc, wsl=wsl, pg_=pg_: e.matmul(
                    ps[pg_][:], wsl[:, kc * 128:(kc + 1) * 128], xn[:, kc, :], start=(kc == 0), stop=(kc == KC - 1)),
                    r=[wn, ("xn", kc)], w=[*pn(pg_)])
            for kc in range(KC):
                S.add("pe", lambda e, kc=kc, wsl=wsl, pu_=pu_: e.matmul(
                    ps[pu_][:], wsl[:, 1024 + kc * 128:1024 + (kc + 1) * 128], xn[:, kc, :],
                    start=(kc == 0), stop=(kc == KC - 1)),
                    r=[wn, ("xn", kc)], w=[*pn(pu_)])
            j = nxt("sg", 2)
            S.add("act", lambda e, j=j, pg_=pg_: e.activation(out=sg[j][:], in_=ps[pg_][:], func=AF.Silu),
                  r=[*pn(pg_)], w=[("sg", j)])
            S.add("dve", lambda e, j=j, pu_=pu_, fc=fc: e.tensor_tensor(out=act[:, fc, :], in0=ps[pu_][:], in1=sg[j][:], op=ALU.mult),
                  r=[*pn(pu_), ("sg", j)], w=[("act", fc)])
        cv_hook()
        S.tag = hname + ".ffn_down"
        for dc in range(KC):
            pd_ = 4 + dc % 2
            for hf in range(2):
                wsl, wn = use_slab("d", dc * 2 + hf)
                for j in range(11):
                    f = hf * 11 + j
                    S.add("pe", lambda e, j=j, f=f, wsl=wsl, pd_=pd_: e.matmul(
                        ps[pd_][:], wsl[:, j * 128:(j + 1) * 128], act[:, f, :], start=(f == 0), stop=(f == FC - 1)),
                        r=[wn, ("act", f)], w=[*pn(pd_)])
            if dc >= 1 and ILSTAT:
                tg_ = S.tag
                S.tag = hname + ".ffn_post/stats"
                stat_mm(dc - 1)
                S.tag = tg_
            S.add("act", lambda e, dc=dc, pd_=pd_: e.activation(out=fo[:, dc, :], in_=ps[pd_][:], func=AF.Copy, scale=g5col(l, n_post, dc)),
                  r=[*pn(pd_), "c:g5"], w=[("fo", dc)])
            S.add("act", lambda e, dc=dc, pd_=pd_: e.activation(out=sq[:, dc, :], in_=ps[pd_][:], func=AF.Square),
                  r=[*pn(pd_)], w=[("sq", dc)])
        cv_hook()
        S.tag = hname + ".ffn_post"
        postnorm_update(hbuf, hname, l, n_post, True, done=(KC - 1 if ILSTAT else 0))

    def tok_ap(dr, t0):
        return dr[:, t0:t0 + TT].rearrange("(c p) t -> p c t", p=128)

    def hs_names(t0):
        return [("hs", t0 // 256), ("hs", t0 // 256 + 1)]

    def load_hA(l, t):
        t0 = 256 * l + TT * t
        src = xT if l == 0 else hs
        S.add("sp", lambda e: e.dma_start(out=hA[:], in_=tok_ap(src, t0)),
              r=([] if l == 0 else hs_names(t0)), w=[("hA", c) for c in range(KC)], dma="hA")

    def phase_A(l, t, preloaded=False, prenormed=False, mid_hook=None):
        t0 = 256 * l + TT * t
        slot = t % 2
        if not preloaded:
            load_hA(l, t)
        ffn(hA, "hA", l, N_FFN1_PRE, N_FFN1_POST, skip_pre=prenormed, mid_hook=mid_hook)
        def store_A():
            tg = S.tag
            S.tag = "st"
            S.add("sp", lambda e: e.dma_start(out=tok_ap(hs, t0), in_=hA[:]),
                  r=[("hA", c) for c in range(KC)], w=hs_names(t0), dma="stA")
            S.tag = tg
        cv_hook()
        S.tag = "A.mixpre"
        prenorm(hA, "hA", l, N_MIX_PRE)
        S.tag = "A.qk"
        for si in range(13):
            if si == 5:
                store_A()
            wsl, wn = use_slab("fm", si)
            pb = 4 + si % 2
            for kc in range(KC):
                S.add("pe", lambda e, kc=kc, wsl=wsl, pb=pb: e.matmul(
                    ps[pb][:], wsl[:, kc * 128:(kc + 1) * 128], xn[:, kc, :], start=(kc == 0), stop=(kc == KC - 1)),
                    r=[wn, ("xn", kc)], w=[*pn(pb)])
            if si < 4:
                dst, dn = QW[slot][:, :, si, :], ("QT", slot, si)
            elif si == 4:
                dst, dn = KT[slot][:, 0, :], ("KT", slot, 0)
            elif si < 9:
                dst, dn = QT[slot][:, si - 1, :], ("QT", slot, si - 1)
            else:
                dst, dn = KT[slot][:, si - 8, :], ("KT", slot, si - 8)
            eng = "act" if si % 2 == 0 else "dve"
            srcp = ps[pb][:].rearrange("p (s q) -> p s q", s=4) if si < 4 else ps[pb][:]
            if eng == "act":
                S.add("act", lambda e, dst=dst, srcp=srcp: e.activation(out=dst, in_=srcp, func=AF.Copy),
                      r=[*pn(pb)], w=[dn])
            else:
                S.add("dve", lambda e, dst=dst, srcp=srcp: e.tensor_copy(out=dst, in_=srcp),
                      r=[*pn(pb)], w=[dn])
        cnt = 0
        S.tag = "A.v"
        for g in range(3):
            wsl, wn = use_slab("wv", g)
            ncol = 256 if g < 2 else 128
            for sub in range(4):
                pb = 4 + cnt % 2
                cnt += 1
                for kc in range(KC):
                    S.add("pe", lambda e, kc=kc, wsl=wsl, pb=pb, sub=sub, ncol=ncol: e.matmul(
                        ps[pb][:, 0:ncol], xn[:, kc, sub * 128:(sub + 1) * 128], wsl[:, kc * ncol:(kc + 1) * ncol],
                        start=(kc == 0), stop=(kc == KC - 1)),
                        r=[wn, ("xn", kc)], w=[*pn(pb)])
                nh = ncol // 64
                h0 = 2 + 4 * g if g < 2 else 0
                dst = V[slot][:, sub, h0:h0 + nh, 0:64]
                srcp = ps[pb][:, 0:ncol].rearrange("p (h d) -> p h d", d=64)
                if cnt % 2 == 0:
                    S.add("act", lambda e, dst=dst, srcp=srcp: e.activation(out=dst, in_=srcp, func=AF.Copy),
                          r=[*pn(pb)], w=[("V", slot, sub, g)])
                else:
                    S.add("dve", lambda e, dst=dst, srcp=srcp: e.tensor_copy(out=dst, in_=srcp),
                          r=[*pn(pb)], w=[("V", slot, sub, g)])

    def build_G(l):
        S.tag = "G"
        S.add("sp", lambda e: e.dma_start(out=R2[:], in_=rpb_d[l * 120:(l + 1) * 120, :]), w=["R2"], dma="r2")
        texp = act[:, 0:15, :].rearrange("p a b -> p (a b)").rearrange("p (m q) -> p m q", q=64)
        for q4 in range(16):
            pb = q4 % 2
            for k in range(4):
                qc = q4 * 4 + k
                S.add("pe", lambda e, qc=qc, k=k, pb=pb: e.transpose(
                    ps[pb][:, k * 120:(k + 1) * 120], R2[:, 63 - qc:191 - qc], identf[0:120, 0:120]),
                    r=["R2", "c:identf"], w=[*pn(pb)])
                S.add("pe", lambda e, qc=qc, k=k, pb=pb: e.transpose(
                    ps[pb][0:64, k * 120:(k + 1) * 120], R2[:, 127 - qc:191 - qc], identf[0:120, 0:120]),
                    r=["R2", "c:identf"], w=[*pn(pb)])
            dst = texp[:, :, q4 * 4:q4 * 4 + 4].rearrange("p m q -> p q m")
            srcp = ps[pb][:, 0:480].rearrange("p (q m) -> p q m", m=120)
            S.add("act", lambda e, dst=dst, srcp=srcp: e.activation(out=dst, in_=srcp, func=AF.Exp),
                  r=[*pn(pb)], w=[("act", c) for c in range(15)])
        Gv = Gt[:].rearrange("p (o h i q) -> p o h i q", o=NOFF, h=8, i=2)
        tv = act[:, 0:15, :].rearrange("p a b -> p (a b)").rearrange("p (h r q) -> p h r q", h=8, r=15)
        k = 0
        for oi in range(NOFF):
            o = oi - 3
            for i_ in range(2):
                for j in range(2):
                    dr = 2 * o + j - i_ + 7
                    psl = slice(64 * j, 64 * j + 64)
                    dst = Gv[psl, oi, :, i_, :]
                    eng = "dve"
                    k += 1
                    if 0 <= dr <= 14:
                        srcp = tv[psl, :, dr, :]
                        cm = cmask[psl, :].unsqueeze(1).to_broadcast([64, 8, 64])
                        S.add(eng, lambda e, dst=dst, srcp=srcp, cm=cm: e.tensor_tensor(out=dst, in0=srcp, in1=cm, op=ALU.mult),
                              r=[("act", c) for c in range(15)] + ["c:cmask"], w=[("G", oi, i_, j)])
                    else:
                        S.add(eng, lambda e, dst=dst: e.memset(dst, 0.0), w=[("G", oi, i_, j)])

    def attn_finish(l, typ, banks, obase):
        i = nxt("On", 2)
        tg = S.tag
        den = small[:, 16 * typ:16 * typ + 8]
        rden = small[:, 16 * typ + 8:16 * typ + 16]
        for b in range(2):
            pv = ps[banks[b]][:, 0:260].rearrange("p (h d) -> p h d", d=65)
            if typ == 0:
                S.add("dve", lambda e, pv=pv, b=b: e.tensor_tensor(
                    out=den[:, 4 * b:4 * b + 4].unsqueeze(2), in0=pv[:, :, 64:65],
                    in1=esink[:, l * 8 + 4 * b:l * 8 + 4 * b + 4].unsqueeze(2), op=ALU.add),
                    r=[*pn(banks[b]), "c:esink"], w=[("den", typ, b)])
            else:
                S.add("dve", lambda e, pv=pv, b=b: e.tensor_scalar(
                    out=den[:, 4 * b:4 * b + 4].unsqueeze(2), in0=pv[:, :, 64:65], scalar1=1e-30, scalar2=None, op0=ALU.max),
                    r=[*pn(banks[b])], w=[("den", typ, b)])
        S.add("dve", lambda e: e.reciprocal(out=rden, in_=den), r=[("den", typ, 0), ("den", typ, 1)], w=[("rden", typ)])
        for b in range(2):
            pv = ps[banks[b]][:, 0:260].rearrange("p (h d) -> p h d", d=65)
            S.add("dve", lambda e, pv=pv, b=b, i=i: e.tensor_tensor(
                out=On[i][:, 256 * b:256 * b + 256].rearrange("p (h d) -> p h d", d=64), in0=pv[:, :, 0:64],
                in1=rden[:, 4 * b:4 * b + 4].unsqueeze(2).to_broadcast([128, 4, 64]), op=ALU.mult),
                r=[*pn(banks[b]), ("rden", typ)], w=[("On", i, b)])
        ssq = small[:, 32 + 4 * typ:32 + 4 * typ + 1]
        vv = small[:, 32 + 4 * typ + 1:32 + 4 * typ + 2]
        rs = small[:, 32 + 4 * typ + 2:32 + 4 * typ + 3]
        def s1():
            S.tag = tg
            S.add("act", lambda e, i=i: e.activation(out=sq[:, 0, :], in_=On[i][:], func=AF.Square, accum_out=ssq),
                  r=[("On", i, 0), ("On", i, 1)], w=[("ssq", typ), ("sq", 0)])

        def s2():
            S.tag = tg
            S.add("dve", lambda e: e.tensor_scalar(out=vv, in0=ssq, scalar1=1.0 / 512, scalar2=EPS, op0=ALU.mult, op1=ALU.add),
                  r=[("ssq", typ)], w=[("vv", typ)])

        def s3():
            S.tag = tg
            S.add("pool", lambda e: e.tensor_tensor(out=rs, in0=vv, in1=nhalf[:, 0:1], op=ALU.pow),
                  r=[("vv", typ), "c:nh"], w=[("rs", typ)])

        def s4():
            S.tag = tg
            S.add("act", lambda e, i=i: e.activation(out=Obf[:, obase:obase + 512], in_=On[i][:], func=AF.Copy, scale=rs),
                  r=[("On", i, 0), ("On", i, 1), ("rs", typ)], w=[("Obf", typ)])
        return [(1, s1), (2, s2), (3, s3), (4, s4)]


    BP = os.environ.get("K_BP", "wWnNt")
    NAP = os.environ.get("K_NAP", "emv")
    NAENG = os.environ.get("K_NAENG", "dve")

    def phase_B(l, t, last, pre=None, tail_hook=None):
        t0 = 256 * (l + 1) + TT * t
        if pre is not None:
            pre()
        S.add("sp", lambda e: e.dma_start(out=hB[:], in_=tok_ap(hs, t0)),
              r=hs_names(t0), w=[("hB", c) for c in range(KC)], dma="hB")
        if True:
            S.add("sp", lambda e: e.dma_start(out=pTt[:], in_=pbf[l * 256:(l + 1) * 256, t0:t0 + TT].rearrange("(c p) t -> p c t", p=128)),
                  r=[("pbf", l)], w=["pTt"], dma="pTt")
        items = []
        pres, blk_first = [], []
        psT = ps[7].bitcast(BF16)
        ggi = (l * 9 + N_GROUP) * 8

        def kv_loc(m):
            a = m - 2 * l
            return (a // 4) % 2, a % 4

        for jb in range(4):
            n = 2 * (l + 1) + 4 * t + jb
            qa = n - 2 * l
            qslot, qsub = (qa // 4) % 2, qa % 4
            qs = slice(qsub * 128, qsub * 128 + 128)
            zb = nxt("Z", NZ)

            def pre(qslot=qslot, qsub=qsub, qs=qs, zb=zb):
                S.tag = "B.qz"
                for h_ in range(2):
                    hp = slice(64 * h_, 64 * h_ + 64)
                    S.add("act", lambda e, hp=hp, h_=h_: e.activation(
                        out=QZw[zb][hp, h_, :].rearrange("p (c q) -> p c q", c=4), in_=QW[qslot][hp, qsub, :, :], func=AF.Copy),
                        r=[("QT", qslot, c) for c in range(4)], w=[("QZw", zb)])
                    S.add("act", lambda e, hp=hp, h_=h_: e.activation(
                        out=QZn[zb][hp, :, h_, :], in_=QT[qslot][hp, 4:8, qs], func=AF.Copy),
                        r=[("QT", qslot, c) for c in range(4, 8)], w=[("QZn", zb)])
            pres.append(pre)
            blk_first.append(len(items))
            for g in range(2):
                hp = slice(64 * g, 64 * g + 64)
                for rel in range(3):
                    m = n - 1 + rel
                    ks, ksub = kv_loc(m)
                    it = {}
                    st = {}

                    def S_(it=it, st=st, ks=ks, ksub=ksub, g=g, zb=zb):
                        S.tag = "B.win"
                        pS = 4 + nxt("SB", NSB)
                        st["pS"] = pS
                        S.add("pe", lambda e: e.matmul(
                            ps[pS], KT[ks][:, 0, ksub * 128:ksub * 128 + 128], QZw[zb][:, g, :], start=True, stop=True),
                            r=[("KT", ks, 0), ("QZw", zb)], w=pn(pS))

                    def SM_(st=st, m=m, rel=rel, g=g):
                        pS = st["pS"]
                        ei, pi = nxt("E", NE), nxt("P", NP_)
                        st["pi"] = pi
                        S.add("act", lambda e: e.activation(
                            out=Eb[ei][:], in_=ps[pS], func=AF.Exp, scale=HD ** -0.5, bias=wkb[:, m:m + 1]),
                            r=pn(pS) + ["c:wkb"], w=[("E", ei)])
                        tb = wt_tab[:, (rel * 8 + 4 * g) * 128:(rel * 8 + 4 * g + 4) * 128]
                        S.add("dve", lambda e: e.tensor_tensor(out=Pb[pi][:], in0=Eb[ei][:], in1=tb, op=ALU.mult),
                              r=[("E", ei), "c:wtab"], w=[("P", pi)])

                    def PV_(st=st, ks=ks, ksub=ksub, g=g, rel=rel):
                        S.tag = "B.winpv"
                        pi = st["pi"]
                        for c in range(4):
                            S.add("pe", lambda e, c=c: e.matmul(
                                ps[g][:, c * 65:(c + 1) * 65], Pb[pi][:, c * 128:(c + 1) * 128], V[ks][:, ksub, g, :],
                                start=(rel == 0 and c == 0), stop=(rel == 2), skip_group_check=True),
                                r=[("P", pi), ("V", ks, ksub, 2)], w=pn(g))
                    it.update(S=S_, SM=SM_, PV=PV_, post=None)
                    items.append(it)

            def postw(l=l):
                S.tag = "B.winfin"
                return attn_finish(l, 0, (0, 1), 0)
            items[-1]["post"] = postw
            offs = NA_OFFS[n]
            for hg in range(2):
                for oi_k, o in enumerate(offs):
                    m = n + o
                    ks, ksub = kv_loc(m)
                    it = {}
                    st = {}

                    def S_(st=st, ks=ks, ksub=ksub, hg=hg, zb=zb):
                        S.tag = "B.na"
                        pS = 4 + nxt("SB", NSB)
                        st["pS"] = pS
                        for pr in range(2):
                            j = 2 * hg + pr
                            S.add("pe", lambda e, pr=pr, j=j: e.matmul(
                                ps[pS][:, pr * 256:(pr + 1) * 256], KT[ks][:, 1 + j, ksub * 128:ksub * 128 + 128],
                                QZn[zb][:, j, :, :].rearrange("p h q -> p (h q)"), start=True, stop=True),
                                r=[("KT", ks, 1 + j), ("QZn", zb)], w=pn(pS))

                    def SM_(st=st, n=n, o=o, hg=hg):
                        pS = st["pS"]
                        ei, pi = nxt("E", NE), nxt("P", NP_)
                        st["pi"] = pi
                        if NA_SAMEI[n][o + 3]:
                            col = (n * NOFF + (o + 3)) * 2
                            S.add("act", lambda e, col=col: e.activation(
                                out=Eb[ei][:], in_=ps[pS], func=AF.Exp, scale=HD ** -0.5, bias=nab[:, col:col + 1]),
                                r=pn(pS) + ["c:nab"], w=[("E", ei)])
                        for i_ in (range(0) if NA_SAMEI[n][o + 3] else range(2)):
                            col = (n * NOFF + (o + 3)) * 2 + i_
                            srcp = ps[pS].rearrange("p (c i q) -> p c i q", c=4, i=2)[:, :, i_, :]
                            dst = Eb[ei][:].rearrange("p (c i q) -> p c i q", c=4, i=2)[:, :, i_, :]
                            S.add("act", lambda e, srcp=srcp, dst=dst, col=col: e.activation(
                                out=dst, in_=srcp, func=AF.Exp, scale=HD ** -0.5, bias=nab[:, col:col + 1]),
                                r=pn(pS) + ["c:nab"], w=[("E", ei)])
                        gb = Gt[:, ((o + 3) * 8 + 4 * hg) * 128:((o + 3) * 8 + 4 * hg + 4) * 128]
                        S.add("dve", lambda e: e.tensor_tensor(out=Pb[pi][:], in0=Eb[ei][:], in1=gb, op=ALU.mult),
                              r=[("E", ei)] + [("G", o + 3, a_, b_) for a_ in range(2) for b_ in range(2)], w=[("P", pi)])

                    def PV_(st=st, ks=ks, ksub=ksub, hg=hg, oi_k=oi_k, nof=len(offs)):
                        S.tag = "B.napv"
                        pi = st["pi"]
                        for c in range(4):
                            h = 4 * hg + c
                            S.add("pe", lambda e, c=c, h=h: e.matmul(
                                ps[2 + hg][:, c * 65:(c + 1) * 65], Pb[pi][:, c * 128:(c + 1) * 128], V[ks][:, ksub, 2 + h, :],
                                start=(oi_k == 0 and c == 0), stop=(oi_k == nof - 1), skip_group_check=True),
                                r=[("P", pi), ("V", ks, ksub, hg)], w=pn(2 + hg))
                    it.update(S=S_, SM=SM_, PV=PV_, post=None)
                    items.append(it)

            def postn(l=l, jb=jb):
                S.tag = "B.nafin"
                stages = attn_finish(l, 1, (2, 3), 512)

                def tr(jb=jb):
                    S.tag = "B.tr"
                    for c in range(KC):
                        S.add("pe", lambda e, c=c: e.transpose(psT[:, c * 128:(c + 1) * 128], Obf[:, c * 128:(c + 1) * 128], identb[:]),
                              r=[("Obf", c // 4), "c:identb"], w=pn(7))
                    S.add("dve", lambda e: e.tensor_tensor(
                        out=xn[:, :, jb * 128:(jb + 1) * 128], in0=psT.rearrange("p (c t) -> p c t", c=8),
                        in1=gn[:, ggi:ggi + 8].unsqueeze(2).to_broadcast([128, 8, 128]), op=ALU.mult),
                        r=pn(7) + ["c:gn"], w=[("xn", c) for c in range(KC)])
                return stages + [(TRDELAY, tr)]
            items[-1]["post"] = postn
        LA = int(os.environ.get("K_LA", "3"))
        deferred = []
        pres[0]()
        for k in range(len(items) + LA):
            if k < len(items):
                if k in blk_first:
                    bi = blk_first.index(k)
                    if bi + 1 < len(pres) and NZ > 1:
                        pres[bi + 1]()
                    elif NZ == 1 and bi > 0:
                        pres[bi]()
                items[k]["S"]()
                items[k]["SM"]()
            j = k - LA
            if j >= 0:
                items[j]["PV"]()
                for d in deferred:
                    d[0] -= 1
                for d in [d for d in deferred if d[0] <= 0]:
                    d[1]()
                    deferred.remove(d)
                if items[j]["post"] is not None:
                    for dly, fn in items[j]["post"]():
                        deferred.append([dly, fn])
        for d in sorted(deferred, key=lambda d: d[0]):
            d[1]()
        cv_hook(2)
        S.tag = "B.wo"
        for dc in range(KC):
            wsl, wn = use_slab("wo", dc)
            pd_ = 4 + dc % 2
            for kc in range(KC):
                S.add("pe", lambda e, kc=kc, wsl=wsl, pd_=pd_: e.matmul(
                    ps[pd_][:], wsl[:, kc * 128:(kc + 1) * 128], xn[:, kc, :], start=(kc == 0), stop=(kc == KC - 1)),
                    r=[wn, ("xn", kc)], w=[*pn(pd_)])
            if dc >= 1 and ILSTAT:
                S.tag = "B.wo_post/stats"
                stat_mm(dc - 1)
                S.tag = "B.wo"
            S.add("act", lambda e, dc=dc, pd_=pd_: e.activation(out=fo[:, dc, :], in_=ps[pd_][:], func=AF.Copy, scale=gcol(l, N_MIX_POST, dc)),
                  r=[*pn(pd_), "c:gn"], w=[("fo", dc)])
            S.add("act", lambda e, dc=dc, pd_=pd_: e.activation(out=sq[:, dc, :], in_=ps[pd_][:], func=AF.Square),
                  r=[*pn(pd_)], w=[("sq", dc)])
        S.tag = "B.wo_post"
        postnorm_update(hB, "hB", l, N_MIX_POST, False, done=(KC - 1 if ILSTAT else 0))
        ffn(hB, "hB", l, N_FFN2_PRE, N_FFN2_POST)
        cv_hook()
        S.tag = "B.ple_pre"
        prenorm(hB, "hB", l, N_PLE_PRE)
        S.tag = "B.ple"
        for fc in range(KC):
            wsl, wn = use_slab("pg", fc)
            pa, pb = fc % 2, 2 + fc % 2
            for kc in range(KC):
                S.add("pe", lambda e, kc=kc, wsl=wsl, pa=pa: e.matmul(
                    ps[pa][:], wsl[:, kc * 128:(kc + 1) * 128], xn[:, kc, :], start=(kc == 0), stop=(kc == KC - 1)),
                    r=[wn, ("xn", kc)], w=[*pn(pa)])
            for kc in range(2):
                S.add("pe", lambda e, kc=kc, wsl=wsl, pb=pb: e.matmul(
                    ps[pb][:], wsl[:, (8 + kc) * 128:(9 + kc) * 128], pTt[:, kc, :], start=(kc == 0), stop=(kc == 1)),
                    r=[wn, "pTt"], w=[*pn(pb)])
            if fc >= 1 and ILSTAT:
                S.tag = "B.ple_post/stats"
                stat_mm(fc - 1)
                S.tag = "B.ple"
            j = nxt("sg", 2)
            S.add("act", lambda e, j=j, pa=pa: e.activation(out=sg[j][:], in_=ps[pa][:], func=AF.Sigmoid),
                  r=[*pn(pa)], w=[("sg", j)])
            S.add("dve", lambda e, j=j, pb=pb, fc=fc: e.tensor_tensor(out=fo[:, fc, :], in0=ps[pb][:], in1=sg[j][:], op=ALU.mult),
                  r=[*pn(pb), ("sg", j)], w=[("fo", fc)])
            S.add("act", lambda e, fc=fc: e.activation(out=sq[:, fc, :], in_=fo[:, fc, :], func=AF.Square),
                  r=[("fo", fc)], w=[("sq", fc)])
            S.add("act", lambda e, fc=fc: e.activation(out=fo[:, fc, :], in_=fo[:, fc, :], func=AF.Copy, scale=gcol(l, N_PLE_POST, fc)),
                  r=[("fo", fc), "c:gn"], w=[("fo", fc)])
        S.tag = "B.ple_post"
        postnorm_update(hB, "hB", l, N_PLE_POST, False, hook=tail_hook, done=(KC - 1 if ILSTAT else 0))
        def store_B():
            tg = S.tag
            S.tag = "st"
            if last:
                y0 = t0 - HALO
                S.add("sp", lambda e: e.dma_start(out=tok_ap(yT, y0), in_=hB[:]),
                      r=[("hB", c) for c in range(KC)], w=[("yT", t)], dma="stB")
            else:
                S.add("sp", lambda e: e.dma_start(out=tok_ap(hs, t0), in_=hB[:]),
                      r=[("hB", c) for c in range(KC)], w=hs_names(t0), dma="stB")
            S.tag = tg
        return store_B

    cur_l = -1
    preloaded = set()
    prenormed = set()
    pending_store = None
    DEFERST = os.environ.get("K_DEFERST", "1") == "1"
    EARLYPRE = os.environ.get("K_EARLYPRE", "1") == "1"
    for idx, (ph, l, t) in enumerate(order):
        if l != cur_l:
            build_G(l)
            cur_l = l
        if ph == "A":
            phase_A(l, t, preloaded=((l, t) in preloaded), prenormed=((l, t) in prenormed and EARLYPRE), mid_hook=pending_store)
            pending_store = None
        else:
            pre = None
            hook = None
            if idx + 1 < len(order) and order[idx + 1][0] == "A":
                _, l2, t2 = order[idx + 1]
                preloaded.add((l2, t2))
                pre = (lambda l2=l2, t2=t2: load_hA(l2, t2))
                if EARLYPRE:
                    prenormed.add((l2, t2))

                    def hook(l2=l2):
                        S.tag = "hA.ffn_pre"
                        prenorm(hA, "hA", l2, N_FFN1_PRE)
            st_fn = phase_B(l, t, (l == NL - 1), pre, hook)
            if idx + 1 < len(order) and order[idx + 1][0] == "A" and DEFERST:
                pending_store = st_fn
            else:
                st_fn()
    if stage == "G":
        build_G(0)
    full = (n_layers == NL and stage is None)
    outs = [("yT", t) for t in range(8)] if full else []
    S.add("sp", lambda e: e.nop(), r=outs + [("hs", k) for k in range(NLOC // 256)], final=True)
    if os.environ.get("K_DUMPTAGS"):
        with open(os.environ["K_DUMPTAGS"], "w") as f:
            for op in S.ops:
                if op["eng"] == "pe":
                    f.write(op["tag"] + "\n")
    S.emit(es)
    es.close()
    return nc


def _slab(W):
    K, Fd = W.shape
    return np.ascontiguousarray(W.reshape(K // 128, 128, Fd // 128, 128).transpose(2, 1, 0, 3))


_QA, _KA, _VA, _QB, _KB, _VB = 0, 512, 640, 768, 1280, 1792


def _fm_cols():
    cols = []
    for c in range(4):
        cols += list(range(_QA + c * 64, _QA + c * 64 + 64)) + list(range(_QA + (4 + c) * 64, _QA + (4 + c) * 64 + 64))
    cols += list(range(_KA, _KA + 128))
    cols += list(range(_QB, _QB + 512))
    cols += list(range(_KB, _KB + 512))
    return np.array(cols)


def _pack_weights(inp):
    out = np.empty((NL, 2, GRP), np.float32)
    fmc = _fm_cols()
    for l in range(NL):
        parts = []
        sg_, su_ = _slab(inp["w_ffn1_gate"][l]), _slab(inp["w_ffn1_up"][l])
        for fc in range(FC):
            parts.append(np.stack([sg_[fc], su_[fc]], axis=1).reshape(128, -1))
        sd_ = _slab(inp["w_ffn1_down"][l])
        for dc in range(8):
            for hf in range(2):
                parts.append(sd_[dc][:, hf * 11:(hf + 1) * 11, :].reshape(128, -1))
        sf_ = _slab(inp["w_in"][l][:, fmc])
        for si in range(13):
            parts.append(sf_[si].reshape(128, -1))
        wv = inp["w_in"][l]
        for cols in (slice(_VB, _VB + 256), slice(_VB + 256, _VB + 512), slice(_VA, _VA + 128)):
            m = wv[:, cols]
            parts.append(np.ascontiguousarray(m.reshape(8, 128, -1).transpose(1, 0, 2)).reshape(128, -1))
        out[l, 0] = np.concatenate([p.reshape(-1) for p in parts])
        parts = []
        so_ = _slab(inp["w_out"][l])
        for dc in range(8):
            parts.append(so_[dc].reshape(128, -1))
        sg_, su_ = _slab(inp["w_ffn2_gate"][l]), _slab(inp["w_ffn2_up"][l])
        for fc in range(FC):
            parts.append(np.stack([sg_[fc], su_[fc]], axis=1).reshape(128, -1))
        sd_ = _slab(inp["w_ffn2_down"][l])
        for dc in range(8):
            for hf in range(2):
                parts.append(sd_[dc][:, hf * 11:(hf + 1) * 11, :].reshape(128, -1))
        spg, spp = _slab(inp["w_ple_gate"][l]), _slab(inp["w_ple_proj"][l])
        for fc in range(8):
            parts.append(np.concatenate([spg[fc], spp[fc]], axis=1).reshape(128, -1))
        out[l, 1] = np.concatenate([p.reshape(-1) for p in parts])
    return out.reshape(-1)


CHUNKS = [("prompt", 0, 0, 8192), ("prompt", 0, 4096, 8192), ("prompt", 1, 0, 8192), ("prompt", 1, 4096, 8192),
          ("sample", 0, 0, 16384), ("sample", 0, 4096, 16384), ("sample", 0, 8192, 16384), ("sample", 0, 12288, 16384)]


def _na_valid_rows(r, rows):
    if 0 <= r < rows:
        s = min(max(r - 4, 0), rows - 8)
    else:
        s = r - 4
    return s, s + 8


def _core_tables(a, Lseq):
    rows = Lseq // 64
    nab = np.full((128, NPAIR, NOFF, 2), NEG, np.float32)
    valid_any = np.zeros((NPAIR, NOFF), bool)
    for n in range(NPAIR):
        for i_ in range(2):
            r = (a - HALO) // 64 + 2 * n + i_
            s, e_ = _na_valid_rows(r, rows)
            for oi in range(NOFF):
                for j in range(2):
                    kr = (a - HALO) // 64 + 2 * (n + oi - 3) + j
                    ok = s <= kr < e_
                    if 0 <= r < rows and not (0 <= kr < rows):
                        ok = False
                    if ok:
                        nab[64 * j:64 * j + 64, n, oi, i_] = 0.0
                        valid_any[n, oi] = True
    wkb = np.full((128, NPAIR), NEG, np.float32)
    for b in range(NPAIR):
        g = a - HALO + 128 * b
        if 0 <= g < Lseq:
            wkb[:, b] = 0.0
    return nab.reshape(128, -1), wkb, valid_any


def _na_offsets():
    va = np.zeros((NPAIR, NOFF), bool)
    same = np.ones((NPAIR, NOFF), bool)
    for (_, _, a, Ls) in CHUNKS:
        nb_, _, v_ = _core_tables(a, Ls)
        va |= v_
        nb_ = nb_.reshape(128, NPAIR, NOFF, 2)
        same &= (nb_[:, :, :, 0] == nb_[:, :, :, 1]).all(axis=0)
    global NA_SAMEI
    NA_SAMEI = same
    offs = []
    for n in range(NPAIR):
        o = [oi - 3 for oi in range(NOFF) if va[n, oi] and 0 <= n + oi - 3 < NPAIR]
        offs.append(o)
    return offs


NA_SAMEI = None
NA_OFFS = _na_offsets()


def _const_tables():
    slopes = np.exp2(-(8.0 / 8) * np.arange(1, 9, dtype=np.float32)).astype(np.float32)
    j = np.arange(128)[:, None, None, None]
    rel = np.arange(3)[None, :, None, None]
    i_ = np.arange(128)[None, None, None, :]
    dist = np.abs((rel - 1) * 128 + j - i_).astype(np.float32)
    tab = np.exp(-slopes[None, None, :, None] * dist) * (dist <= 128)
    wtab = tab.astype(np.float32).reshape(128, -1)
    c = np.arange(64)
    cs = np.clip(c - 8, 0, 48)
    col_ok = (c[None, :] >= cs[:, None]) & (c[None, :] < cs[:, None] + 16)
    cm = col_ok.T.astype(np.float32)
    cmask = np.concatenate([cm, cm], axis=0)
    return wtab, cmask


_PROG = {}


def _prep_inputs(inp):
    inp = {k: np.asarray(v) for k, v in inp.items()}
    wall = _pack_weights(inp)
    gn = np.ascontiguousarray(inp["norm_g"].reshape(NL, 9, 8, 128).transpose(3, 0, 1, 2)).reshape(128, -1)
    sinkb = np.ascontiguousarray(np.broadcast_to(inp["sink"].reshape(1, -1), (128, NL * 8))).astype(np.float32)
    rp = np.zeros((NL, 8, 15, 127), np.float32)
    rp[..., 48:79] = inp["rpb"]
    rp = rp.reshape(NL * 120, 127)
    rpbp = np.concatenate([np.zeros((NL * 120, 64), np.float32), rp], axis=1)
    wtab, cmask = _const_tables()
    identf = np.eye(128, dtype=np.float32)
    shared = dict(wall=wall, gn=gn.astype(np.float32), sinkb=sinkb, rpbp=rpbp, wtab=wtab, cmask=cmask, identf=identf)
    maps = []
    for (kind, b, a, Ls) in CHUNKS:
        x = inp["x_" + kind][b]
        p = inp["p_" + kind][:, b]
        lo, hi = a - HALO, a + NTOK + HALO
        slo, shi = max(lo, 0), min(hi, Ls)
        xl = np.zeros((NLOC, D), np.float32)
        xl[slo - lo:shi - lo] = x[slo:shi]
        pl = np.zeros((NL, NLOC, 256), np.float32)
        pl[:, slo - lo:shi - lo] = p[:, slo:shi]
        nab, wkb, _ = _core_tables(a, Ls)
        m = dict(shared)
        m["xT"] = np.ascontiguousarray(xl.T)
        m["pT"] = np.ascontiguousarray(pl.transpose(0, 2, 1)).reshape(NL * 256, NLOC)
        m["nab"] = nab
        m["wkb"] = wkb
        maps.append(m)
    return maps


def kernel(**inputs):
    maps = _prep_inputs(inputs)
    if "nc" not in _PROG:
        _PROG["nc"] = build_program()
    res = run_bass_kernel_spmd(_PROG["nc"], maps, core_ids=list(range(8)))
    y_prompt = np.empty((2, 8192, D), np.float32)
    y_sample = np.empty((1, 16384, D), np.float32)
    for ci, (kind, b, a, Ls) in enumerate(CHUNKS):
        y = np.asarray(res.results[ci]["yT"]).T
        (y_prompt if kind == "prompt" else y_sample)[b, a:a + NTOK] = y
    return (y_prompt, y_sample)
```
